# Optimizing a Trainium2 kernel written in Bass

```python
import math
import jax, jax.numpy as jnp
from jax import lax
import numpy as np

D_MODEL = 1024
BATCH = 8
SEQ = 4096
DEPTH = 4

A_HEAD_DIM = 64
A_Q_HEADS = 8
A_KV_HEADS = 2
WINDOW = 128
ROPE_THETA = 10000.0
B_HEADS = 4
B_HEAD_DIM = 128
B_CONV = 4
B_CHUNK = 64
LRU_WIDTH = D_MODEL
LRU_BLOCKS = 4
LRU_CONV = 4
LRU_C = 8.0
D_FF = 4 * D_MODEL
A_Q_W = A_Q_HEADS * A_HEAD_DIM
A_KV_W = A_KV_HEADS * A_HEAD_DIM
B_W = B_HEADS * B_HEAD_DIM
HYB_SPLITS = (A_Q_W, A_KV_W, A_KV_W, 3 * B_W, B_W, B_HEADS, B_HEADS)
HYB_PROJ = A_Q_W + 2 * A_KV_W + 4 * B_W + 2 * B_HEADS
MIX_W = A_Q_W + B_W
N_HYB = (DEPTH + 1) // 2
N_REC = DEPTH // 2
DN_ALPHA = (2 * DEPTH) ** 0.25
DN_BETA = (8 * DEPTH) ** -0.25
LN_EPS = 1e-5
NORM_EPS = 1e-6

kernel_name = 'hybrid_swa_deltanet_rglru_deepnorm_trunk'


def split_cols(t, sizes):
    out, start = [], 0
    for s in sizes:
        out.append(t[..., start:start + s])
        start += s
    return out


def layer_norm(x, g, b):
    xf = x.astype(jnp.float32)
    mu = jnp.mean(xf, axis=-1, keepdims=True)
    var = jnp.mean(jnp.square(xf - mu), axis=-1, keepdims=True)
    y = (xf - mu) * lax.rsqrt(var + LN_EPS) * g.astype(jnp.float32) + b.astype(jnp.float32)
    return y.astype(x.dtype)


def l2_normalize(x):
    xf = x.astype(jnp.float32)
    return xf * lax.rsqrt(jnp.sum(xf * xf, axis=-1, keepdims=True) + NORM_EPS)


def apply_rope(x):
    T, dh = x.shape[1], x.shape[-1]
    half = dh // 2
    inv_freq = ROPE_THETA ** (-jnp.arange(half, dtype=jnp.float32) / half)
    ang = jnp.arange(T, dtype=jnp.float32)[:, None] * inv_freq[None, :]
    cos = jnp.cos(ang)[None, :, None, :]
    sin = jnp.sin(ang)[None, :, None, :]
    xf = x.astype(jnp.float32)
    x1, x2 = xf[..., :half], xf[..., half:]
    return jnp.concatenate([x1 * cos - x2 * sin, x2 * cos + x1 * sin], axis=-1).astype(x.dtype)


def causal_dwconv(x, w, b=None):
    K = w.shape[0]
    T = x.shape[1]
    xp = jnp.pad(x, ((0, 0), (K - 1, 0), (0, 0)))
    y = sum(xp[:, j:j + T] * w[j] for j in range(K))
    if b is not None:
        y = y + b
    return y


def sliding_window_attention(q, k, v, sinks):
    f32 = jnp.float32
    bsz, T, HQ, DH = q.shape
    HKV = k.shape[2]
    G = HQ // HKV
    W = WINDOW
    NB = T // W
    qb = q.reshape(bsz, NB, W, HKV, G, DH)

    def band(t):
        prev = jnp.pad(t, ((0, 0), (W, 0), (0, 0), (0, 0)))[:, :T]
        return jnp.concatenate([prev.reshape(bsz, NB, W, HKV, DH),
                                t.reshape(bsz, NB, W, HKV, DH)], axis=2)

    kb, vb = band(k), band(v)
    s = jnp.einsum('bnqhgd,bnkhd->bnhgqk', qb, kb).astype(f32) * (DH ** -0.5)
    dist = (jnp.arange(W)[:, None] + W) - jnp.arange(2 * W)[None, :]
    in_window = (dist >= 0) & (dist < W)
    key_pos = (jnp.arange(NB)[:, None] - 1) * W + jnp.arange(2 * W)[None, :]
    mask = in_window[None] & (key_pos >= 0)[:, None, :]
    s = jnp.where(mask[None, :, None, None], s, -jnp.inf)
    sink = jnp.broadcast_to(sinks.astype(f32).reshape(1, 1, HKV, G, 1, 1), s.shape[:-1] + (1,))
    p = jax.nn.softmax(jnp.concatenate([s, sink], axis=-1), axis=-1)[..., :-1]
    o = jnp.einsum('bnhgqk,bnkhd->bnqhgd', p.astype(v.dtype), vb)
    return o.reshape(bsz, T, HQ * DH)


def gated_delta_rule(q, k, v, g, beta):
    f32 = jnp.float32
    bsz, T, H, DK = q.shape
    DV = v.shape[-1]
    C = B_CHUNK
    N = T // C

    def chunk(t):
        t = t.astype(f32).reshape((bsz, N, C, H) + t.shape[3:])
        return jnp.moveaxis(t, 3, 1)

    q = chunk(q) * (DK ** -0.5)
    k = chunk(k)
    v = chunk(v)
    g = jnp.cumsum(chunk(g), axis=-1)
    beta = chunk(beta)
    k_beta = k * beta[..., None]
    v_beta = v * beta[..., None]
    idx = jnp.arange(C)
    incl = idx[:, None] >= idx[None, :]
    strict = idx[:, None] > idx[None, :]
    diff = g[..., :, None] - g[..., None, :]
    decay_incl = jnp.exp(jnp.where(incl, diff, -jnp.inf))
    decay_strict = jnp.where(strict, decay_incl, 0.0)
    a_mat = jnp.einsum('bhnik,bhnjk->bhnij', k_beta, k) * decay_strict
    eye = jnp.eye(C, dtype=f32)
    t_mat = lax.linalg.triangular_solve(a_mat + eye, jnp.broadcast_to(eye, a_mat.shape),
                                        left_side=True, lower=True, unit_diagonal=True)
    u = jnp.einsum('bhnij,bhnjv->bhniv', t_mat, v_beta)
    w = jnp.einsum('bhnij,bhnjk->bhnik', t_mat, k_beta * jnp.exp(g)[..., None])
    qk = jnp.einsum('bhnik,bhnjk->bhnij', q, k) * decay_incl
    q_g = q * jnp.exp(g)[..., None]
    g_last = g[..., -1]
    k_tail = k * jnp.exp(g_last[..., None] - g)[..., None]
    xs = tuple(jnp.moveaxis(t, 2, 0) for t in (q_g, k_tail, u, w, qk, g_last))

    def step(S, inp):
        q_c, kt_c, u_c, w_c, qk_c, gl_c = inp
        v_new = u_c - jnp.einsum('bhck,bhkv->bhcv', w_c, S)
        o_c = jnp.einsum('bhck,bhkv->bhcv', q_c, S) + jnp.einsum('bhij,bhjv->bhiv', qk_c, v_new)
        S = S * jnp.exp(gl_c)[..., None, None] + jnp.einsum('bhck,bhcv->bhkv', kt_c, v_new)
        return S, o_c

    S0 = jnp.zeros((bsz, H, DK, DV), f32)
    _, o = lax.scan(step, S0, xs)
    return jnp.transpose(o, (1, 0, 3, 2, 4)).reshape(bsz, T, H, DV)


def hybrid_mixer(x, w_in, sinks, conv_w, a_log, dt_bias, norm_w, w_out):
    f32 = jnp.float32
    bsz, T, _ = x.shape
    proj = x @ w_in
    qa, ka, va, qkv_b, z, beta_logit, a_logit = split_cols(proj, HYB_SPLITS)
    qa = apply_rope(qa.reshape(bsz, T, A_Q_HEADS, A_HEAD_DIM))
    ka = apply_rope(ka.reshape(bsz, T, A_KV_HEADS, A_HEAD_DIM))
    va = va.reshape(bsz, T, A_KV_HEADS, A_HEAD_DIM)
    o_a = sliding_window_attention(qa, ka, va, sinks)
    qkv_b = jax.nn.silu(causal_dwconv(qkv_b, conv_w))
    qb, kb, vb = split_cols(qkv_b, (B_W, B_W, B_W))
    qb = l2_normalize(qb.reshape(bsz, T, B_HEADS, B_HEAD_DIM))
    kb = l2_normalize(kb.reshape(bsz, T, B_HEADS, B_HEAD_DIM))
    vb = vb.reshape(bsz, T, B_HEADS, B_HEAD_DIM)
    beta = jax.nn.sigmoid(beta_logit.astype(f32))
    g = -jnp.exp(a_log.astype(f32)) * jax.nn.softplus(a_logit.astype(f32) + dt_bias.astype(f32))
    o_b = gated_delta_rule(qb, kb, vb, g, beta)
    o_b = o_b * lax.rsqrt(jnp.mean(o_b * o_b, axis=-1, keepdims=True) + NORM_EPS) * norm_w.astype(f32)
    o_b = o_b * jax.nn.silu(z.astype(f32).reshape(bsz, T, B_HEADS, B_HEAD_DIM))
    o_b = o_b.reshape(bsz, T, B_W).astype(x.dtype)
    return jnp.concatenate([o_a, o_b], axis=-1) @ w_out


def rg_lru(x, w_a, b_a, w_x, b_x, lam):
    f32 = jnp.float32
    bsz, T, W = x.shape
    xh = x.reshape(bsz, T, LRU_BLOCKS, W // LRU_BLOCKS)
    r = jax.nn.sigmoid((jnp.einsum('bthi,hij->bthj', xh, w_a).reshape(bsz, T, W) + b_a).astype(f32))
    i = jax.nn.sigmoid((jnp.einsum('bthi,hij->bthj', xh, w_x).reshape(bsz, T, W) + b_x).astype(f32))
    log_a = -LRU_C * r * jax.nn.softplus(-lam.astype(f32))
    a = jnp.exp(log_a)
    b = jnp.sqrt(-jnp.expm1(2.0 * log_a)) * (i * x.astype(f32))

    def combine(c1, c2):
        a1, b1 = c1
        a2, b2 = c2
        return a1 * a2, a2 * b1 + b2

    _, h = lax.associative_scan(combine, (a, b), axis=1)
    return h.astype(x.dtype)


def recurrent_mixer(x, w_in, conv_w, conv_b, w_a, b_a, w_x, b_x, lam, w_out):
    proj = x @ w_in
    xr, gate = split_cols(proj, (LRU_WIDTH, LRU_WIDTH))
    xr = causal_dwconv(xr, conv_w, conv_b)
    h = rg_lru(xr, w_a, b_a, w_x, b_x, lam)
    return (h * jax.nn.gelu(gate)) @ w_out


def sqrelu_mlp(x, w1, w2):
    h = jax.nn.relu(x @ w1)
    return (h * h) @ w2


def setup_inputs(seed: int = 0) -> dict:
    key = jax.random.key(seed)
    ks = jax.random.split(key, 24)
    f32 = jnp.float32

    def nrm(k, shape, scale):
        return jax.random.normal(k, shape, f32) * scale

    bw = LRU_WIDTH // LRU_BLOCKS
    x = nrm(ks[0], (BATCH, SEQ, D_MODEL), 1.0)
    hyb_w_in = nrm(ks[1], (N_HYB, D_MODEL, HYB_PROJ), D_MODEL ** -0.5)
    hyb_sinks = nrm(ks[2], (N_HYB, A_Q_HEADS), 0.5)
    hyb_conv_w = nrm(ks[3], (N_HYB, B_CONV, 3 * B_W), B_CONV ** -0.5)
    hyb_a_log = jnp.log(jax.random.uniform(ks[4], (N_HYB, B_HEADS), f32, 1.0, 16.0))
    dt = jnp.exp(jax.random.uniform(ks[5], (N_HYB, B_HEADS), f32, math.log(1e-3), math.log(1e-1)))
    hyb_dt_bias = dt + jnp.log(-jnp.expm1(-dt))
    hyb_norm_w = 1.0 + nrm(ks[6], (N_HYB, B_HEAD_DIM), 0.02)
    hyb_w_out = nrm(ks[7], (N_HYB, MIX_W, D_MODEL), DN_BETA * MIX_W ** -0.5)
    rec_w_in = nrm(ks[8], (N_REC, D_MODEL, 2 * LRU_WIDTH), D_MODEL ** -0.5)
    rec_conv_w = nrm(ks[9], (N_REC, LRU_CONV, LRU_WIDTH), LRU_CONV ** -0.5)
    rec_conv_b = nrm(ks[10], (N_REC, LRU_WIDTH), 0.02)
    rec_w_a = nrm(ks[11], (N_REC, LRU_BLOCKS, bw, bw), bw ** -0.5)
    rec_b_a = nrm(ks[12], (N_REC, LRU_WIDTH), 0.02)
    rec_w_x = nrm(ks[13], (N_REC, LRU_BLOCKS, bw, bw), bw ** -0.5)
    rec_b_x = nrm(ks[14], (N_REC, LRU_WIDTH), 0.02)
    u = jax.random.uniform(ks[15], (N_REC, LRU_WIDTH), f32, 0.9, 0.999)
    a0 = u ** (1.0 / LRU_C)
    rec_lambda = jnp.log(a0) - jnp.log1p(-a0)
    rec_w_out = nrm(ks[16], (N_REC, LRU_WIDTH, D_MODEL), DN_BETA * LRU_WIDTH ** -0.5)
    ln1_g = 1.0 + nrm(ks[17], (DEPTH, D_MODEL), 0.02)
    ln1_b = nrm(ks[18], (DEPTH, D_MODEL), 0.02)
    mlp_w1 = nrm(ks[19], (DEPTH, D_MODEL, D_FF), D_MODEL ** -0.5)
    mlp_w2 = nrm(ks[20], (DEPTH, D_FF, D_MODEL), DN_BETA * D_FF ** -0.5)
    ln2_g = 1.0 + nrm(ks[21], (DEPTH, D_MODEL), 0.02)
    ln2_b = nrm(ks[22], (DEPTH, D_MODEL), 0.02)
    return {'x': x, 'hyb_w_in': hyb_w_in, 'hyb_sinks': hyb_sinks, 'hyb_conv_w': hyb_conv_w,
            'hyb_a_log': hyb_a_log, 'hyb_dt_bias': hyb_dt_bias, 'hyb_norm_w': hyb_norm_w,
            'hyb_w_out': hyb_w_out, 'rec_w_in': rec_w_in, 'rec_conv_w': rec_conv_w,
            'rec_conv_b': rec_conv_b, 'rec_w_a': rec_w_a, 'rec_b_a': rec_b_a, 'rec_w_x': rec_w_x,
            'rec_b_x': rec_b_x, 'rec_lambda': rec_lambda, 'rec_w_out': rec_w_out,
            'ln1_g': ln1_g, 'ln1_b': ln1_b, 'mlp_w1': mlp_w1, 'mlp_w2': mlp_w2,
            'ln2_g': ln2_g, 'ln2_b': ln2_b}


def reference(x, hyb_w_in, hyb_sinks, hyb_conv_w, hyb_a_log, hyb_dt_bias, hyb_norm_w, hyb_w_out,
              rec_w_in, rec_conv_w, rec_conv_b, rec_w_a, rec_b_a, rec_w_x, rec_b_x, rec_lambda,
              rec_w_out, ln1_g, ln1_b, mlp_w1, mlp_w2, ln2_g, ln2_b):
    for layer in range(DEPTH):
        j = layer // 2
        if layer % 2 == 0:
            mix = hybrid_mixer(x, hyb_w_in[j], hyb_sinks[j], hyb_conv_w[j], hyb_a_log[j],
                               hyb_dt_bias[j], hyb_norm_w[j], hyb_w_out[j])
        else:
            mix = recurrent_mixer(x, rec_w_in[j], rec_conv_w[j], rec_conv_b[j], rec_w_a[j],
                                  rec_b_a[j], rec_w_x[j], rec_b_x[j], rec_lambda[j], rec_w_out[j])
        x = layer_norm(DN_ALPHA * x + mix, ln1_g[layer], ln1_b[layer])
        x = layer_norm(DN_ALPHA * x + sqrelu_mlp(x, mlp_w1[layer], mlp_w2[layer]), ln2_g[layer], ln2_b[layer])
    return x
```

```python
import math
from contextlib import ExitStack

import numpy as np
import concourse.bass as bass
import concourse.mybir as mybir
from concourse.bass_utils import run_bass_kernel_spmd

F32 = mybir.dt.float32
BF16 = mybir.dt.bfloat16
AF = mybir.ActivationFunctionType
ALU = mybir.AluOpType

D = 1024
DFF = 4096
DEPTH = 4
SEQ = 4096
NB = 8
DN_ALPHA = (2 * DEPTH) ** 0.25
LN_EPS = 1e-5
NORM_EPS = 1e-6
HYBW = 3464
DEBUG_STAGE = 2
LAST_CNT = None


class Buf:
    __slots__ = ("t", "w", "r", "name")

    def __init__(self, t=None, name=""):
        self.t = t
        self.w = None
        self.r = {}
        self.name = name

    def __getitem__(self, idx):
        return self.t[idx]


class KB:
    def __init__(self, nc, es):
        self.nc = nc
        self.es = es
        self.engs = {"pe": nc.tensor, "act": nc.scalar, "dve": nc.vector, "pool": nc.gpsimd, "sp": nc.sync}
        self.sem = {}
        self.cnt = {}
        self.seen = {k: {} for k in self.engs}
        for k in self.engs:
            self.sem[k] = es.enter_context(nc.semaphore("s_" + k))
            self.cnt[k] = 0
        self.dpool = {}
        for q in ("sp", "pool"):
            sems = [es.enter_context(nc.semaphore("d_%s%d" % (q, i))) for i in range(24)]
            self.dpool[q] = {"sems": sems, "cum": [0] * len(sems), "i": 0}
        self.nbuf = 0
        self.pe_sems = {id(self.sem['pe'])}

    def sb(self, shape, dt, name=None, es=None):
        self.nbuf += 1
        t = (es or self.es).enter_context(self.nc.sbuf_tensor("%s_%d" % (name or "sb", self.nbuf), list(shape), dt))
        return Buf(t, name)

    def barrier(self):
        toks = [(self.sem[k], self.cnt[k]) for k in self.engs if self.cnt[k]]
        for q, p in self.dpool.items():
            for s_, c in zip(p["sems"], p["cum"]):
                if c:
                    toks.append((s_, c))
        for e in self.engs:
            for t in toks:
                if t[0] is self.sem[e]:
                    continue
                self._wait(e, t)

    def psum(self, shape, dt=F32, name=None):
        self.nbuf += 1
        t = self.es.enter_context(self.nc.psum_tensor("%s_%d" % (name or "ps", self.nbuf), list(shape), dt))
        return Buf(t, name)

    def dram(self, name, shape, dt, kind="Internal"):
        return self.nc.dram_tensor(name, list(shape), dt, kind=kind).ap()

    def _wait(self, e, tok):
        if tok is None:
            return
        sem, val = tok
        d = self.seen[e]
        key = id(sem)
        if d.get(key, 0) >= val:
            return
        self.engs[e].wait_ge(sem, val)
        d[key] = val

    def _deps(self, e, rd, wr):
        for b in rd:
            if e == "pe" and b.w is not None and id(b.w[0]) in self.pe_sems:
                continue
            self._wait(e, b.w)
        for b in wr:
            if not (e == "pe" and b.w is not None and id(b.w[0]) in self.pe_sems):
                self._wait(e, b.w)
            for k, t in b.r.items():
                if e == "pe" and k == "pe":
                    continue
                self._wait(e, t)

    def _commit(self, e, tok, rd, wr):
        for b in rd:
            b.r[e] = tok
        for b in wr:
            b.w = tok
            b.r = {}

    def op(self, e, fn, rd=(), wr=()):
        if self.cnt[e] >= 24000:
            self.nrot = getattr(self, "nrot", 0) + 1
            self.sem[e] = self.es.enter_context(self.nc.semaphore("s_%s_r%d" % (e, self.nrot)))
            self.cnt[e] = 0
            if e == "pe":
                self.pe_sems.add(id(self.sem[e]))
        self._deps(e, rd, wr)
        inst = fn(self.engs[e])
        self.cnt[e] += 1
        inst.then_inc(self.sem[e], 1)
        tok = (self.sem[e], self.cnt[e])
        self._commit(e, tok, rd, wr)
        return tok

    def dma(self, q, out, in_, rd=(), wr=(), **kw):
        self._deps(q, rd, wr)
        p = self.dpool[q]
        i = p["i"] % len(p["sems"])
        p["i"] += 1
        s = p["sems"][i]
        self._wait(q, (s, p["cum"][i]) if p["cum"][i] else None)
        inst = self.engs[q].dma_start(out=out, in_=in_, **kw)
        inst.then_inc(s, 16)
        p["cum"][i] += 16
        tok = (s, p["cum"][i])
        for b in rd:
            b.r["dma_" + q + str(i)] = tok
        for b in wr:
            b.w = tok
            b.r = {}
        return tok

    def finish(self):
        for q, p in self.dpool.items():
            for s, c in zip(p["sems"], p["cum"]):
                if c:
                    self._wait("sp", (s, c))


def build_program(T, kinds, debug=None):
    nc = bass.Bass("TRN2", target_bir_lowering=False)
    L = len(kinds)
    n_h = max(1, sum(1 for k in kinds if k == "hyb"))
    n_r = max(1, sum(1 for k in kinds if k == "rec"))
    NT = T // 512
    NS = T // 128

    def din(name, shape, dt=F32):
        return nc.dram_tensor(name, list(shape), dt, kind="ExternalInput").ap()

    x_in = din("x", [T, D])
    ident_in = din("ident", [128, 128])
    hyb_w_in = din("hyb_w_in", [n_h, D, HYBW])
    hyb_w_out = din("hyb_w_out", [n_h, D, D])
    hyb_convw = din("hyb_convw", [n_h, 128, 12, 4])
    hyb_small = din("hyb_small", [n_h, 128, 16])
    hyb_normw = din("hyb_normw", [n_h, 128, 1])
    rope_cs = din("rope_cs", [2, 128, T])
    masks_in = din("masks", [4, 128, 512])
    tri_in = din("tri", [128, 128])
    rec_w_in = din("rec_w_in", [n_r, D, 2 * D])
    rec_w_out = din("rec_w_out", [n_r, D, D])
    rec_w_ax = din("rec_w_ax", [n_r, 2, 4, 256, 256])
    rec_vec = din("rec_vec", [n_r, 128, 8, 8])
    ln_gb = din("ln_gb", [L, 4, 128, D])
    mlp_w1 = din("mlp_w1", [L, D, DFF])
    mlp_w2 = din("mlp_w2", [L, DFF, D])
    out = nc.dram_tensor("out", [T, D], F32, kind="ExternalOutput").ap()

    es = ExitStack()
    with es:
        kb = KB(nc, es)
        xT_d = kb.dram("xT_d", [8, 128, T], BF16)
        x1T_d = kb.dram("x1T_d", [8, 128, T], BF16)
        xres_d = kb.dram("xres_d", [T, D], F32)
        x1_d = kb.dram("x1_d", [T, D], F32)
        p_d = kb.dram("p_d", [T, D], F32)
        B_xT = [Buf(name="xT%d" % i) for i in range(NT)]
        B_x1T = [Buf(name="x1T%d" % i) for i in range(NT)]
        B_xres = [Buf(name="xres%d" % i) for i in range(NS)]
        B_x1 = [Buf(name="x1_%d" % i) for i in range(NS)]
        B_p = [Buf(name="p%d" % i) for i in range(NS)]
        B_out = [Buf(name="out%d" % i) for i in range(NS)]

        ident = kb.sb([128, 128], F32, "ident")
        kb.dma("sp", ident[:], ident_in[:, :], wr=[ident])
        identb = kb.sb([128, 128], BF16, "identb")
        kb.op("act", lambda e: e.activation(out=identb[:], in_=ident[:], func=AF.Copy), rd=[ident], wr=[identb])

        ps = [kb.psum([128, 512], F32, "ps%d" % i) for i in range(8)]

        gb = [kb.sb([128, D], F32, "gb%d" % i) for i in range(2)]
        xtile = [kb.sb([128, 8, 512], BF16, "xtile%d" % i) for i in range(2)]
        tm_in = [kb.sb([128, D], F32, "tmin%d" % i) for i in range(3)]
        r_sb = [kb.sb([128, D], F32, "r%d" % i) for i in range(2)]
        xo_sb = [kb.sb([128, D], F32, "xo%d" % i) for i in range(2)]
        xoT = [kb.sb([128, 8, 512], BF16, "xoT%d" % i) for i in range(1)]
        st6 = [kb.sb([128, 12], F32, "st%d" % i) for i in range(2)]
        mv = [kb.sb([128, 8], F32, "mv%d" % i) for i in range(2)]
        mhalf = kb.sb([128, 1], F32, "mhalf")
        kb.op("pool", lambda e: e.memset(mhalf[:], -0.5), wr=[mhalf])
        mhalf4 = kb.sb([128, 4], F32, "mhalf4")
        kb.op("pool", lambda e: e.memset(mhalf4[:], -0.5), wr=[mhalf4])
        cnt = {"ln": 0, "tm": 0}

        def wchunk(wreg):
            return Buf(wreg.t, "wchunk")

        def load_w(dst_ap, src_ap, wb):
            kb.dma("pool", dst_ap, src_ap, wr=[wb])

        def layer_norm(r, gbuf, bbuf):
            i = cnt["ln"] % 2
            cnt["ln"] += 1
            s6, m, xo = st6[i], mv[i], xo_sb[i]
            kb.op("dve", lambda e: e.bn_stats(out=s6[:, 0:6], in_=r[:, 0:512]), rd=[r], wr=[s6])
            kb.op("dve", lambda e: e.bn_stats(out=s6[:, 6:12], in_=r[:, 512:1024]), rd=[r], wr=[s6])
            kb.op("dve", lambda e: e.bn_aggr(out=m[:, 0:2], in_=s6[:, 0:12]), rd=[s6], wr=[m])
            kb.op("pool", lambda e: e.tensor_scalar(out=m[:, 2:3], in0=m[:, 1:2], scalar1=LN_EPS, scalar2=None,
                                                    op0=ALU.add), rd=[m], wr=[m])
            kb.op("pool", lambda e: e.tensor_tensor(out=m[:, 3:4], in0=m[:, 2:3], in1=mhalf[:], op=ALU.pow),
                  rd=[m, mhalf], wr=[m])
            kb.op("dve", lambda e: e.scalar_tensor_tensor(out=m[:, 4:5], in0=m[:, 0:1], scalar=-1.0, in1=m[:, 3:4],
                                                          op0=ALU.mult, op1=ALU.mult), rd=[m], wr=[m])
            kb.op("act", lambda e: e.activation(out=r[:], in_=r[:], func=AF.Identity, bias=m[:, 4:5],
                                                scale=m[:, 3:4]), rd=[r, m], wr=[r])
            kb.op("pool", lambda e: e.tensor_tensor(out=r[:], in0=r[:], in1=gbuf[:], op=ALU.mult),
                  rd=[r, gbuf], wr=[r])
            kb.op("pool", lambda e: e.tensor_tensor(out=xo[:], in0=r[:], in1=bbuf[:], op=ALU.add),
                  rd=[r, bbuf], wr=[xo])
            return xo

        def transpose_to_fm(xo, dst, s, pbanks):
            for half in range(2):
                pb = pbanks[half]
                for j in range(4):
                    kc = half * 4 + j
                    kb.op("pe", lambda e: e.transpose(
                        out=pb[:, j * 128:(j + 1) * 128], in_=xo[:, kc * 128:(kc + 1) * 128], identity=ident[:]),
                        rd=[xo, ident], wr=[pb])
                o_ap = dst[:, half * 4:half * 4 + 4, s * 128:(s + 1) * 128]
                i_ap = pb[:, :].rearrange("p (j t) -> p j t", j=4)
                if half == 0:
                    kb.op("act", lambda e: e.activation(out=o_ap, in_=i_ap, func=AF.Copy), rd=[pb], wr=[dst])
                else:
                    kb.op("dve", lambda e: e.tensor_copy(out=o_ap, in_=i_ap), rd=[pb], wr=[dst])

        def pass_t0():
            for i in range(NT):
                dst = xoT[0]
                for s in range(4):
                    g = i * 4 + s
                    xin = tm_in[cnt["tm"] % 3]
                    cnt["tm"] += 1
                    kb.dma("sp", xin[:], x_in[g * 128:(g + 1) * 128, :], wr=[xin])
                    transpose_to_fm(xin, dst, s, (ps[0 + 2 * (s % 2)], ps[1 + 2 * (s % 2)]))
                kb.dma("sp", xT_d[:, :, i * 512:(i + 1) * 512].rearrange("k p t -> p k t"), dst[:], rd=[dst],
                       wr=[B_xT[i]])
            kb.barrier()

        def pass_mlp(l, half, last):
            with ExitStack() as pes:
                wreg = kb.sb([128, 32768], BF16, "wreg", es=pes)
                hT = [kb.sb([128, 512], BF16, "hT%d" % f, es=pes) for f in range(16)]
                relu_t = [kb.sb([128, 512], F32, "relu%d" % f, es=pes) for f in range(2)]
                w1v = wreg[:, 0:8 * 2048].rearrange("p (k f) -> p k f", k=8)
                w2v = wreg[:, 8 * 2048:8 * 2048 + 16 * 1024].rearrange("p (f d) -> p f d", f=16)
                f0 = half * 2048
                W1 = [wchunk(wreg) for _ in range(8)]
                W2 = [wchunk(wreg) for _ in range(16)]
                for k in range(8):
                    load_w(w1v[:, k, :], mlp_w1[l, k * 128:(k + 1) * 128, f0:f0 + 2048], W1[k])
                for f in range(16):
                    load_w(w2v[:, f, :], mlp_w2[l, f0 + f * 128:f0 + (f + 1) * 128, :], W2[f])
                if half == 1:
                    kb.dma("sp", gb[0][:], ln_gb[l, 2], wr=[gb[0]])
                    kb.dma("sp", gb[1][:], ln_gb[l, 3], wr=[gb[1]])
                for i in range(NT):
                    xt = xtile[i % 2]
                    kb.dma("sp", xt[:], x1T_d[:, :, i * 512:(i + 1) * 512].rearrange("k p t -> p k t"),
                           rd=[B_x1T[i]], wr=[xt])
                    for f in range(16):
                        pb = ps[f % 2]
                        for k in range(8):
                            kb.op("pe", lambda e: e.matmul(
                                pb[:], lhsT=w1v[:, k, f * 128:(f + 1) * 128], rhs=xt[:, k, :],
                                start=(k == 0), stop=(k == 7)), rd=[W1[k], xt], wr=[pb])
                        rt = relu_t[f % 2]
                        kb.op("act", lambda e: e.activation(out=rt[:], in_=pb[:], func=AF.Relu), rd=[pb], wr=[rt])
                        kb.op("dve", lambda e: e.scalar_tensor_tensor(
                            out=hT[f][:], in0=pb[:], scalar=0.0, in1=rt[:], op0=ALU.max, op1=ALU.mult),
                            rd=[pb, rt], wr=[hT[f]])
                    dstT = xoT[0]
                    for s in range(4):
                        g = i * 4 + s
                        pbs = (ps[2 + 2 * (s % 2)], ps[3 + 2 * (s % 2)])
                        for hh in range(2):
                            for f in range(16):
                                kb.op("pe", lambda e: e.matmul(
                                    pbs[hh][:], lhsT=hT[f][:, s * 128:(s + 1) * 128],
                                    rhs=w2v[:, f, hh * 512:(hh + 1) * 512], start=(f == 0), stop=(f == 15)),
                                    rd=[W2[f], hT[f]], wr=[pbs[hh]])
                        xin = tm_in[cnt["tm"] % 3]
                        cnt["tm"] += 1
                        r = r_sb[g % 2]
                        if half == 0:
                            kb.dma("sp", xin[:], x1_d[g * 128:(g + 1) * 128, :], rd=[B_x1[g]], wr=[xin])
                            for hh in range(2):
                                kb.op("dve", lambda e: e.scalar_tensor_tensor(
                                    out=r[:, hh * 512:(hh + 1) * 512], in0=xin[:, hh * 512:(hh + 1) * 512],
                                    scalar=float(DN_ALPHA), in1=pbs[hh][:], op0=ALU.mult, op1=ALU.add),
                                    rd=[xin, pbs[hh]], wr=[r])
                            kb.dma("sp", p_d[g * 128:(g + 1) * 128, :], r[:], rd=[r], wr=[B_p[g]])
                        else:
                            kb.dma("sp", xin[:], p_d[g * 128:(g + 1) * 128, :], rd=[B_p[g]], wr=[xin])
                            for hh in range(2):
                                kb.op("dve", lambda e: e.tensor_tensor(
                                    out=r[:, hh * 512:(hh + 1) * 512], in0=xin[:, hh * 512:(hh + 1) * 512],
                                    in1=pbs[hh][:], op=ALU.add), rd=[xin, pbs[hh]], wr=[r])
                            xo = layer_norm(r, gb[0], gb[1])
                            if last:
                                kb.dma("sp", out[g * 128:(g + 1) * 128, :], xo[:], rd=[xo], wr=[B_out[g]])
                            else:
                                kb.dma("sp", xres_d[g * 128:(g + 1) * 128, :], xo[:], rd=[xo], wr=[B_xres[g]])
                                transpose_to_fm(xo, dstT, s, (ps[6], ps[7]))
                    if half == 1 and not last:
                        kb.dma("sp", xT_d[:, :, i * 512:(i + 1) * 512].rearrange("k p t -> p k t"), dstT[:],
                               rd=[dstT], wr=[B_xT[i]])
                kb.barrier()

        def outproj_ln1(l, i, mix, wov, WO, x_src, Bx_src, nsub=4):
            dstT = xoT[0]
            TW = nsub * 128
            for s in range(nsub):
                g = i * nsub + s
                pbs = (ps[4], ps[5])
                for hh in range(2):
                    for k in range(8):
                        kb.op("pe", lambda e: e.matmul(
                            pbs[hh][:], lhsT=mix[:, k, s * 128:(s + 1) * 128],
                            rhs=wov[:, k, hh * 512:(hh + 1) * 512], start=(k == 0), stop=(k == 7)),
                            rd=[WO[k], mix], wr=[pbs[hh]])
                xin = tm_in[cnt["tm"] % 3]
                cnt["tm"] += 1
                if Bx_src is None:
                    kb.dma("sp", xin[:], x_src[g * 128:(g + 1) * 128, :], wr=[xin])
                else:
                    kb.dma("sp", xin[:], x_src[g * 128:(g + 1) * 128, :], rd=[Bx_src[g]], wr=[xin])
                r = r_sb[g % 2]
                for hh in range(2):
                    kb.op("dve", lambda e: e.scalar_tensor_tensor(
                        out=r[:, hh * 512:(hh + 1) * 512], in0=xin[:, hh * 512:(hh + 1) * 512],
                        scalar=float(DN_ALPHA), in1=pbs[hh][:], op0=ALU.mult, op1=ALU.add),
                        rd=[xin, pbs[hh]], wr=[r])
                xo = layer_norm(r, gb[0], gb[1])
                kb.dma("sp", x1_d[g * 128:(g + 1) * 128, :], xo[:], rd=[xo], wr=[B_x1[g]])
                transpose_to_fm(xo, dstT, s, (ps[6], ps[7]))
            if nsub == 4:
                kb.dma("sp", x1T_d[:, :, i * 512:(i + 1) * 512].rearrange("k p t -> p k t"), dstT[:], rd=[dstT],
                       wr=[B_x1T[i]])
            else:
                kb.dma("sp", x1T_d[:, :, i * TW:(i + 1) * TW].rearrange("k p t -> p k t"), dstT[:, :, 0:TW],
                       rd=[dstT], wr=[B_x1T[(i * TW) // 512]])

        def pass_rec(l, j, x_src, Bx_src):
            with ExitStack() as pes:
                wreg = kb.sb([128, 28672], BF16, "wreg", es=pes)
                mix = kb.sb([128, 8, 512], BF16, "mix", es=pes)
                rvec = kb.sb([128, 8, 8], F32, "rvec", es=pes)
                rtmp = kb.sb([128, 8, 8], F32, "rtmp", es=pes)
                c8 = kb.sb([128, 8], F32, "c8", es=pes)
                hcar = kb.sb([128, 8], F32, "hcar", es=pes)
                cbuf = [kb.sb([128, 515], F32, "cbuf%d" % k, es=pes) for k in range(8)]
                xcf = [kb.sb([128, 512], F32, "xcf%d" % k, es=pes) for k in range(8)]
                xcb = [kb.sb([128, 512], BF16, "xcb%d" % k, es=pes) for k in range(8)]
                gl = [kb.sb([128, 512], F32, "gl%d" % k, es=pes) for k in range(2)]
                g_r = [kb.sb([128, 512], F32, "g_r%d" % k, es=pes) for k in range(2)]
                g_i = [kb.sb([128, 512], F32, "g_i%d" % k, es=pes) for k in range(2)]
                g_a = [kb.sb([128, 512], F32, "g_a%d" % k, es=pes) for k in range(2)]
                g_b = [kb.sb([128, 512], F32, "g_b%d" % k, es=pes) for k in range(2)]
                winv = wreg[:, 0:8 * 2048].rearrange("p (k c) -> p k c", k=8)
                o1 = 8 * 2048
                wov = wreg[:, o1:o1 + 8 * 1024].rearrange("p (k c) -> p k c", k=8)
                o2 = o1 + 8 * 1024
                waxv = wreg[:, o2:o2 + 16 * 256].rearrange("p (a h k c) -> p a h k c", a=2, h=4, k=2)
                WI = [wchunk(wreg) for _ in range(8)]
                WO = [wchunk(wreg) for _ in range(8)]
                WAX = [wchunk(wreg) for _ in range(2)]
                for k in range(8):
                    load_w(winv[:, k, :], rec_w_in[j, k * 128:(k + 1) * 128, :], WI[k])
                for a in range(2):
                    load_w(waxv[:, a], rec_w_ax[j, a].rearrange("h (k p) c -> p h k c", p=128), WAX[a])
                for k in range(8):
                    load_w(wov[:, k, :], rec_w_out[j, k * 128:(k + 1) * 128, :], WO[k])
                kb.dma("sp", gb[0][:], ln_gb[l, 0], wr=[gb[0]])
                kb.dma("sp", gb[1][:], ln_gb[l, 1], wr=[gb[1]])
                kb.dma("sp", rvec[:], rec_vec[j], wr=[rvec])
                lam = rvec[:, :, 7]
                sp_ = rtmp
                V = "dve"

                def vop(fn):
                    kb.op(V, fn, rd=[rvec, rtmp], wr=[rtmp])

                vop(lambda e: e.tensor_scalar(out=sp_[:, 0, :], in0=lam, scalar1=-1.0, scalar2=None, op0=ALU.mult))
                vop(lambda e: e.tensor_tensor(out=sp_[:, 1, :], in0=sp_[:, 0, :], in1=lam, op=ALU.min))
                kb.op("act", lambda e: e.activation(out=sp_[:, 2, :], in_=sp_[:, 1, :], func=AF.Exp), rd=[rtmp],
                      wr=[rtmp])
                vop(lambda e: e.tensor_scalar(out=sp_[:, 3, :], in0=sp_[:, 2, :], scalar1=2.0, scalar2=None,
                                              op0=ALU.add))
                vop(lambda e: e.reciprocal(out=sp_[:, 3, :], in_=sp_[:, 3, :]))
                vop(lambda e: e.tensor_tensor(out=sp_[:, 3, :], in0=sp_[:, 3, :], in1=sp_[:, 2, :], op=ALU.mult))
                vop(lambda e: e.tensor_tensor(out=sp_[:, 4, :], in0=sp_[:, 3, :], in1=sp_[:, 3, :], op=ALU.mult))
                vop(lambda e: e.tensor_scalar(out=sp_[:, 5, :], in0=sp_[:, 4, :], scalar1=1.0 / 15,
                                              scalar2=1.0 / 13, op0=ALU.mult, op1=ALU.add))
                for c in (11, 9, 7, 5, 3, 1):
                    vop(lambda e: e.tensor_tensor(out=sp_[:, 5, :], in0=sp_[:, 5, :], in1=sp_[:, 4, :], op=ALU.mult))
                    vop(lambda e: e.tensor_scalar(out=sp_[:, 5, :], in0=sp_[:, 5, :], scalar1=1.0 / c, scalar2=None,
                                                  op0=ALU.add))
                vop(lambda e: e.tensor_tensor(out=sp_[:, 5, :], in0=sp_[:, 5, :], in1=sp_[:, 3, :], op=ALU.mult))
                vop(lambda e: e.tensor_scalar(out=sp_[:, 6, :], in0=sp_[:, 0, :], scalar1=0.0, scalar2=None,
                                              op0=ALU.max))
                vop(lambda e: e.scalar_tensor_tensor(out=sp_[:, 6, :], in0=sp_[:, 5, :], scalar=2.0,
                                                     in1=sp_[:, 6, :], op0=ALU.mult, op1=ALU.add))
                kb.op(V, lambda e: e.tensor_scalar(out=c8[:, :], in0=sp_[:, 6, :], scalar1=-8.0, scalar2=None,
                                                   op0=ALU.mult), rd=[rtmp], wr=[c8])
                for kc in range(8):
                    kb.op("pool", lambda e: e.memset(cbuf[kc][:, 0:3], 0.0), wr=[cbuf[kc]])
                kb.op("pool", lambda e: e.memset(hcar[:], 0.0), wr=[hcar])

                for i in range(NT):
                    xt = xtile[i % 2]
                    kb.dma("sp", xt[:], xT_d[:, :, i * 512:(i + 1) * 512].rearrange("k p t -> p k t"),
                           rd=[B_xT[i]], wr=[xt])

                    def inproj(cc, pb):
                        for k in range(8):
                            kb.op("pe", lambda e: e.matmul(
                                pb[:], lhsT=winv[:, k, cc * 128:(cc + 1) * 128], rhs=xt[:, k, :],
                                start=(k == 0), stop=(k == 7)), rd=[WI[k], xt], wr=[pb])

                    for cc in range(8):
                        pb = ps[cc % 2]
                        inproj(cc, pb)
                        kb.op("act", lambda e: e.activation(out=cbuf[cc][:, 3:515], in_=pb[:], func=AF.Copy),
                              rd=[pb], wr=[cbuf[cc]])
                    for kc in range(8):
                        cb = cbuf[kc]
                        xc = xcf[kc]
                        kb.op("dve", lambda e: e.tensor_scalar(
                            out=xc[:], in0=cb[:, 3:515], scalar1=rvec[:, kc, 3:4], scalar2=rvec[:, kc, 4:5],
                            op0=ALU.mult, op1=ALU.add), rd=[cb, rvec], wr=[xc])
                        for jj in range(3):
                            kb.op("dve", lambda e: e.scalar_tensor_tensor(
                                out=xc[:], in0=cb[:, jj:jj + 512], scalar=rvec[:, kc, jj:jj + 1], in1=xc[:],
                                op0=ALU.mult, op1=ALU.add), rd=[cb, rvec, xc], wr=[xc])
                        kb.op("pool", lambda e: e.tensor_copy(out=cb[:, 0:3], in_=cb[:, 512:515]), rd=[cb], wr=[cb])
                        kb.op("act", lambda e: e.activation(out=xcb[kc][:], in_=xc[:], func=AF.Copy), rd=[xc],
                              wr=[xcb[kc]])
                    for oc in range(8):
                        h = oc // 2
                        cl = (oc % 2) * 128
                        pa, px, pg = ps[2], ps[3], ps[oc % 2]
                        for a, pb in ((0, pa), (1, px)):
                            for k in range(2):
                                kb.op("pe", lambda e: e.matmul(
                                    pb[:], lhsT=waxv[:, a, h, k, cl:cl + 128], rhs=xcb[2 * h + k][:],
                                    start=(k == 0), stop=(k == 1)), rd=[WAX[a], xcb[2 * h + k]], wr=[pb])
                        inproj(8 + oc, pg)
                        rg, ig, av, bv, gg = g_r[oc % 2], g_i[oc % 2], g_a[oc % 2], g_b[oc % 2], gl[oc % 2]
                        kb.op("act", lambda e: e.activation(out=rg[:], in_=pa[:], func=AF.Sigmoid,
                                                            bias=rvec[:, oc, 5:6]), rd=[pa, rvec], wr=[rg])
                        kb.op("act", lambda e: e.activation(out=ig[:], in_=px[:], func=AF.Sigmoid,
                                                            bias=rvec[:, oc, 6:7]), rd=[px, rvec], wr=[ig])
                        kb.op("act", lambda e: e.activation(out=gg[:], in_=pg[:], func=AF.Gelu), rd=[pg], wr=[gg])
                        kb.op("act", lambda e: e.activation(out=av[:], in_=rg[:], func=AF.Exp,
                                                            scale=c8[:, oc:oc + 1]), rd=[rg, c8], wr=[av])
                        kb.op("pool", lambda e: e.tensor_tensor(out=rg[:], in0=av[:], in1=av[:], op=ALU.mult),
                              rd=[av], wr=[rg])
                        kb.op("pool", lambda e: e.tensor_scalar(out=rg[:], in0=rg[:], scalar1=-1.0, scalar2=1.0,
                                                                op0=ALU.mult, op1=ALU.add), rd=[rg], wr=[rg])
                        kb.op("act", lambda e: e.activation(out=rg[:], in_=rg[:], func=AF.Sqrt), rd=[rg], wr=[rg])
                        kb.op("dve", lambda e: e.tensor_tensor(out=bv[:], in0=ig[:], in1=xcf[oc][:], op=ALU.mult),
                              rd=[ig, xcf[oc]], wr=[bv])
                        kb.op("dve", lambda e: e.tensor_tensor(out=bv[:], in0=bv[:], in1=rg[:], op=ALU.mult),
                              rd=[bv, rg], wr=[bv])
                        kb.op("dve", lambda e: e.tensor_tensor_scan(out=ig[:], data0=av[:], data1=bv[:],
                                                                    initial=hcar[:, oc:oc + 1], op0=ALU.mult,
                                                                    op1=ALU.add), rd=[av, bv, hcar], wr=[ig])
                        kb.op("pool", lambda e: e.tensor_copy(out=hcar[:, oc:oc + 1], in_=ig[:, 511:512]), rd=[ig],
                              wr=[hcar])
                        kb.op("pool", lambda e: e.tensor_tensor(out=mix[:, oc, :], in0=ig[:], in1=gg[:],
                                                                op=ALU.mult), rd=[ig, gg], wr=[mix])
                    outproj_ln1(l, i, mix, wov, WO, x_src, Bx_src)
                kb.barrier()

        def pass_hyb(l, j, x_src, Bx_src):
            with ExitStack() as pes:
                TW = 256
                NTH = T // TW

                def lb(shape, dt, name):
                    return kb.sb(shape, dt, name, es=pes)

                wreg = lb([128, 8 * HYBW + 8 * 1024], BF16, "wreg")
                winv = wreg[:, 0:8 * HYBW].rearrange("p (k c) -> p k c", k=8)
                wov = wreg[:, 8 * HYBW:].rearrange("p (k c) -> p k c", k=8)
                WI = [wchunk(wreg) for _ in range(8)]
                WO = [wchunk(wreg) for _ in range(8)]
                hw = HYBW // 2
                for k in range(8):
                    load_w(winv[:, k, 0:hw], hyb_w_in[j, k * 128:(k + 1) * 128, 0:hw], WI[k])
                    load_w(winv[:, k, hw:HYBW], hyb_w_in[j, k * 128:(k + 1) * 128, hw:HYBW], WI[k])
                for k in range(8):
                    load_w(wov[:, k, :], hyb_w_out[j, k * 128:(k + 1) * 128, :], WO[k])
                kb.dma("sp", gb[0][:], ln_gb[l, 0], wr=[gb[0]])
                kb.dma("sp", gb[1][:], ln_gb[l, 1], wr=[gb[1]])
                mk = [lb([128, 512], F32, "mk%d" % i) for i in range(4)]
                for i_ in range(4):
                    kb.dma("sp", mk[i_][:], masks_in[i_], wr=[mk[i_]])
                maskcb = lb([128, 512], BF16, "maskcb")
                maskpb = lb([128, 512], BF16, "maskpb")
                kb.op("act", lambda e: e.activation(out=maskcb[:], in_=mk[0][:], func=AF.Copy), rd=[mk[0]], wr=[maskcb])
                kb.op("act", lambda e: e.activation(out=maskpb[:], in_=mk[1][:], func=AF.Copy), rd=[mk[1]], wr=[maskpb])
                tri = lb([128, 128], F32, "tri")
                kb.dma("sp", tri[:], tri_in[:, :], wr=[tri])
                onesf = lb([128, 128], F32, "onesf")
                onesb = lb([128, 128], BF16, "onesb")
                kb.op("pool", lambda e: e.memset(onesf[:], 1.0), wr=[onesf])
                kb.op("pool", lambda e: e.memset(onesb[:], 1.0), wr=[onesb])
                small = lb([128, 16], F32, "small")
                kb.dma("sp", small[:], hyb_small[j], wr=[small])
                esink = lb([128, 8], F32, "esink")
                nea = lb([128, 4], F32, "nea")
                kb.op("act", lambda e: e.activation(out=esink[:], in_=small[:, 0:8], func=AF.Exp), rd=[small], wr=[esink])
                kb.op("act", lambda e: e.activation(out=nea[:], in_=small[:, 8:12], func=AF.Exp), rd=[small], wr=[nea])
                kb.op("dve", lambda e: e.tensor_scalar(out=nea[:], in0=nea[:], scalar1=-1.0, scalar2=None, op0=ALU.mult),
                      rd=[nea], wr=[nea])
                cw = lb([128, 12, 4], F32, "cw")
                kb.dma("sp", cw[:], hyb_convw[j], wr=[cw])
                nw = lb([128, 1], F32, "nw")
                kb.dma("sp", nw[:], hyb_normw[j], wr=[nw])
                qTt = lb([128, 4, TW], BF16, "qTt")
                kTt = lb([128, 128 + TW], BF16, "kTt")
                vtm = lb([128, 3, 128], BF16, "vtm")
                cb = [lb([128, 3 + TW], F32, "cb%d" % i) for i in range(2)]
                halo = lb([128, 12, 3], F32, "halo")
                kb.op("pool", lambda e: e.memset(halo[:], 0.0), wr=[halo])
                cvo = [lb([128, TW], F32, "cvo%d" % i) for i in range(2)]
                sqb = [lb([128, TW], BF16, "sqb%d" % i) for i in range(2)]
                rnb = [lb([128, TW], F32, "rnb%d" % i) for i in range(2)]
                qbT = [lb([128, TW], BF16, "qbT%d" % i) for i in range(4)]
                kbT = [lb([128, TW], BF16, "kbT%d" % i) for i in range(4)]
                vbT = [lb([128, TW], BF16, "vbT%d" % i) for i in range(4)]
                zs = [lb([128, TW], F32, "zs%d" % i) for i in range(4)]
                mix = lb([128, 8, TW], BF16, "mix")
                cs = [lb([128, 2, TW], F32, "cs%d" % i) for i in range(2)]
                rt1 = lb([128, TW], F32, "rt1")
                rt2 = lb([128, TW], F32, "rt2")
                Ec = lb([128, 512], BF16, "Ec")
                Ep = lb([128, 512], BF16, "Ep")
                rden = lb([128, 512], F32, "rden")
                sc = [lb([128, 40], F32, "sc%d" % i) for i in range(2)]
                dgb = lb([128, 512], F32, "dgb")
                tE = lb([128, 512], F32, "tE")
                QKD = lb([128, 512], BF16, "QKD")
                Um = [lb([128, 512], F32, "Um%d" % i) for i in range(2)]
                Lm = [lb([128, 512], F32, "Lm%d" % i) for i in range(2)]
                Rm = lb([128, 512], F32, "Rm")
                TT = lb([128, 512], BF16, "TT")
                S = lb([128, 512], F32, "S")
                Sb = lb([128, 512], BF16, "Sb")
                kb.op("pool", lambda e: e.memset(S[:], 0.0), wr=[S])
                kb.op("pool", lambda e: e.memset(Sb[:], 0.0), wr=[Sb])
                kTM = lb([128, 512], BF16, "kTM")
                vTM = lb([128, 512], BF16, "vTM")
                rhsv = lb([128, 512], BF16, "rhsv")
                vn = lb([128, 512], BF16, "vn")
                vnd = lb([128, 512], BF16, "vnd")
                P2sb = lb([128, 512], F32, "P2sb")
                ob = lb([128, 512], F32, "ob")
                osq = lb([128, 128], F32, "osq")
                h4 = lambda t_: t_[:, :].rearrange("p (h c) -> p h c", h=4)

                for i in range(NTH):
                    xt = xtile[i % 2]
                    kb.dma("sp", xt[:, :, 0:TW], xT_d[:, :, i * TW:(i + 1) * TW].rearrange("k p t -> p k t"),
                           rd=[B_xT[(i * TW) // 512]], wr=[xt])
                    cst = cs[i % 2]
                    kb.dma("sp", cst[:], rope_cs[:, :, i * TW:(i + 1) * TW].rearrange("a p t -> p a t"), wr=[cst])

                    def proj(c0, ncol, pb, pcols=None):
                        for k in range(8):
                            kb.op("pe", lambda e: e.matmul(
                                pb[0:ncol, 0:TW], lhsT=winv[:, k, c0:c0 + ncol], rhs=xt[:, k, 0:TW],
                                start=(k == 0), stop=(k == 7)), rd=[WI[k], xt], wr=[pb])

                    for c in range(5):
                        p1, p2 = ps[0], ps[1]
                        proj(c * 128, 128, p1)
                        proj(640 + c * 128, 128, p2)
                        kb.op("dve", lambda e: e.tensor_tensor(out=rt1[:], in0=p1[:, 0:TW], in1=cst[:, 0, :],
                                                               op=ALU.mult), rd=[p1, cst], wr=[rt1])
                        kb.op("dve", lambda e: e.tensor_tensor(out=rt2[:], in0=p2[:, 0:TW], in1=cst[:, 1, :],
                                                               op=ALU.mult), rd=[p2, cst], wr=[rt2])
                        if c < 4:
                            kb.op("pool", lambda e: e.tensor_tensor(out=qTt[:, c, :], in0=rt1[:], in1=rt2[:],
                                                                    op=ALU.add), rd=[rt1, rt2], wr=[qTt])
                        else:
                            kb.op("pool", lambda e: e.tensor_tensor(out=kTt[:, 128:128 + TW], in0=rt1[:], in1=rt2[:],
                                                                    op=ALU.add), rd=[rt1, rt2], wr=[kTt])
                    pv = ps[0]
                    for s in range(2):
                        for k in range(8):
                            kb.op("pe", lambda e: e.matmul(
                                pv[:, s * 128:(s + 1) * 128], lhsT=xt[:, k, s * 128:(s + 1) * 128],
                                rhs=winv[:, k, 1280:1408], start=(k == 0), stop=(k == 7)), rd=[WI[k], xt], wr=[pv])
                    kb.op("act", lambda e: e.activation(out=vtm[:, 1:3, :], in_=pv[:, 0:256].rearrange(
                        "p (s c) -> p s c", s=2), func=AF.Copy), rd=[pv], wr=[vtm])
                    pbd = ps[1]
                    for s in range(2):
                        for k in range(8):
                            kb.op("pe", lambda e: e.matmul(
                                pbd[:, s * 8:(s + 1) * 8], lhsT=xt[:, k, s * 128:(s + 1) * 128],
                                rhs=winv[:, k, 3456:3464], start=(k == 0), stop=(k == 7)), rd=[WI[k], xt], wr=[pbd])
                    for s in range(2):
                        c_ = sc[s]
                        kb.op("act", lambda e: e.activation(out=c_[:, 0:4], in_=pbd[:, s * 8:s * 8 + 4],
                                                            func=AF.Sigmoid), rd=[pbd], wr=[c_])
                        kb.op("dve", lambda e: e.tensor_scalar(out=c_[:, 4:8], in0=c_[:, 0:4], scalar1=-1.0,
                                                               scalar2=None, op0=ALU.mult), rd=[c_], wr=[c_])
                        kb.op("dve", lambda e: e.tensor_tensor(out=c_[:, 32:36], in0=pbd[:, s * 8 + 4:s * 8 + 8],
                                                               in1=small[:, 12:16], op=ALU.add),
                              rd=[pbd, small], wr=[c_])
                        sp_chain(c_, 32, 36, 12)
                        kb.op("dve", lambda e: e.tensor_tensor(out=c_[:, 8:12], in0=c_[:, 12:16], in1=nea[:],
                                                               op=ALU.mult), rd=[c_, nea], wr=[c_])
                    for c in range(12):
                        pb = ps[c % 2]
                        proj(1408 + c * 128, 128, pb)
                        cbt = cb[c % 2]
                        kb.op("pool", lambda e: e.tensor_copy(out=cbt[:, 0:3], in_=halo[:, c, :]), rd=[halo],
                              wr=[cbt])
                        kb.op("act", lambda e: e.activation(out=cbt[:, 3:3 + TW], in_=pb[:, 0:TW], func=AF.Copy),
                              rd=[pb], wr=[cbt])
                        co = cvo[c % 2]
                        kb.op("dve", lambda e: e.tensor_scalar(out=co[:], in0=cbt[:, 3:3 + TW], scalar1=cw[:, c, 3:4],
                                                               scalar2=None, op0=ALU.mult), rd=[cbt, cw], wr=[co])
                        for jj in range(3):
                            kb.op("dve", lambda e: e.scalar_tensor_tensor(
                                out=co[:], in0=cbt[:, jj:jj + TW], scalar=cw[:, c, jj:jj + 1], in1=co[:],
                                op0=ALU.mult, op1=ALU.add), rd=[cbt, cw, co], wr=[co])
                        kb.op("pool", lambda e: e.tensor_copy(out=halo[:, c, :], in_=cbt[:, TW:TW + 3]), rd=[cbt],
                              wr=[halo])
                        if c >= 8:
                            kb.op("act", lambda e: e.activation(out=vbT[c - 8][:], in_=co[:], func=AF.Silu),
                                  rd=[co], wr=[vbT[c - 8]])
                            continue
                        kb.op("act", lambda e: e.activation(out=co[:], in_=co[:], func=AF.Silu), rd=[co], wr=[co])
                        sq = sqb[c % 2]
                        kb.op("act", lambda e: e.activation(out=sq[:], in_=co[:], func=AF.Square), rd=[co], wr=[sq])
                        pn = ps[2 + c % 2]
                        kb.op("pe", lambda e: e.matmul(pn[:, 0:TW], lhsT=onesb[:], rhs=sq[:], start=True, stop=True),
                              rd=[onesb, sq], wr=[pn])
                        rn = rnb[c % 2]
                        kb.op("dve", lambda e: e.tensor_scalar(out=rn[:], in0=pn[:, 0:TW], scalar1=NORM_EPS,
                                                               scalar2=None, op0=ALU.add), rd=[pn], wr=[rn])
                        kb.op("act", lambda e: e.activation(out=rn[:], in_=rn[:], func=AF.Ln), rd=[rn], wr=[rn])
                        kb.op("act", lambda e: e.activation(out=rn[:], in_=rn[:], func=AF.Exp, scale=-0.5),
                              rd=[rn], wr=[rn])
                        dstq = qbT[c] if c < 4 else kbT[c - 4]
                        if c < 4:
                            kb.op("dve", lambda e: e.scalar_tensor_tensor(
                                out=dstq[:], in0=co[:], scalar=float(128 ** -0.5), in1=rn[:], op0=ALU.mult,
                                op1=ALU.mult), rd=[co, rn], wr=[dstq])
                        else:
                            kb.op("dve", lambda e: e.tensor_tensor(out=dstq[:], in0=co[:], in1=rn[:], op=ALU.mult),
                                  rd=[co, rn], wr=[dstq])
                    for c in range(4):
                        pb = ps[c % 2]
                        proj(2944 + c * 128, 128, pb)
                        kb.op("act", lambda e: e.activation(out=zs[c][:], in_=pb[:, 0:TW], func=AF.Silu), rd=[pb],
                              wr=[zs[c]])

                    for s in range(2 if DEBUG_STAGE >= 1 else 0):
                        gblk = i * 2 + s
                        for kv in range(2):
                            pr = slice(kv * 64, (kv + 1) * 64)
                            sc_, sp_b, po, pd = ps[2], ps[3], ps[4], ps[5]
                            qv = qTt[pr, :, s * 128:(s + 1) * 128]
                            kb.op("pe", lambda e: e.matmul(
                                sc_[:, :].rearrange("p (j q) -> p j q", j=4), lhsT=kTt[pr, 128 + s * 128:256 + s * 128],
                                rhs=qv, start=True, stop=True), rd=[kTt, qTt], wr=[sc_])
                            kb.op("act", lambda e: e.activation(out=Ec[:], in_=sc_[:], func=AF.Exp, scale=0.125),
                                  rd=[sc_], wr=[Ec])
                            kb.op("pool", lambda e: e.tensor_tensor(out=Ec[:], in0=Ec[:], in1=maskcb[:], op=ALU.mult),
                                  rd=[Ec, maskcb], wr=[Ec])
                            hasp = gblk > 0
                            if hasp:
                                kb.op("pe", lambda e: e.matmul(
                                    sp_b[:, :].rearrange("p (j q) -> p j q", j=4), lhsT=kTt[pr, s * 128:128 + s * 128],
                                    rhs=qv, start=True, stop=True), rd=[kTt, qTt], wr=[sp_b])
                                kb.op("act", lambda e: e.activation(out=Ep[:], in_=sp_b[:], func=AF.Exp, scale=0.125),
                                      rd=[sp_b], wr=[Ep])
                                kb.op("pool", lambda e: e.tensor_tensor(out=Ep[:], in0=Ep[:], in1=maskpb[:],
                                                                        op=ALU.mult), rd=[Ep, maskpb], wr=[Ep])
                            kb.op("pe", lambda e: e.matmul(po[:], lhsT=vtm[:, 1 + s, :], rhs=Ec[:], start=True,
                                                           stop=not hasp), rd=[vtm, Ec], wr=[po])
                            if hasp:
                                kb.op("pe", lambda e: e.matmul(po[:], lhsT=vtm[:, s, :], rhs=Ep[:], start=False,
                                                               stop=True), rd=[vtm, Ep], wr=[po])
                            kb.op("pe", lambda e: e.matmul(pd[:], lhsT=onesb[:], rhs=Ec[:], start=True,
                                                           stop=not hasp), rd=[onesb, Ec], wr=[pd])
                            if hasp:
                                kb.op("pe", lambda e: e.matmul(pd[:], lhsT=onesb[:], rhs=Ep[:], start=False,
                                                               stop=True), rd=[onesb, Ep], wr=[pd])
                            for g_ in range(4):
                                kb.op("dve", lambda e: e.tensor_scalar(
                                    out=rden[pr, g_ * 128:(g_ + 1) * 128], in0=pd[pr, g_ * 128:(g_ + 1) * 128],
                                    scalar1=esink[pr, kv * 4 + g_:kv * 4 + g_ + 1], scalar2=None, op0=ALU.add),
                                    rd=[pd, esink], wr=[rden])
                            kb.op("dve", lambda e: e.reciprocal(out=rden[pr, :], in_=rden[pr, :]), rd=[rden],
                                  wr=[rden])
                            kb.op("dve", lambda e: e.tensor_tensor(
                                out=mix[pr, 0:4, s * 128:(s + 1) * 128],
                                in0=po[pr, :].rearrange("p (j q) -> p j q", j=4),
                                in1=rden[pr, :].rearrange("p (j q) -> p j q", j=4), op=ALU.mult),
                                rd=[po, rden], wr=[mix])
                    kb.op("pool", lambda e: e.tensor_copy(out=kTt[:, 0:128], in_=kTt[:, TW:TW + 128]), rd=[kTt],
                          wr=[kTt])
                    kb.op("pool", lambda e: e.tensor_copy(out=vtm[:, 0, :], in_=vtm[:, 2, :]), rd=[vtm], wr=[vtm])

                    for s in range(2 if DEBUG_STAGE >= 2 else 0):
                        c_ = sc[s]
                        tok = slice(s * 128, (s + 1) * 128)
                        pA, pB, pC, pD = ps[2], ps[3], ps[4], ps[5]
                        pI = (ps[6], ps[7], ps[0], ps[1])
                        kb.op("pe", lambda e: e.matmul(pA[:, 0:4], lhsT=tri[:], rhs=c_[:, 8:12], start=True, stop=True),
                              rd=[tri, c_], wr=[pA])
                        kb.op("pe", lambda e: e.matmul(pB[:, 0:4], lhsT=onesf[:], rhs=c_[:, 8:12], start=True,
                                                       stop=True), rd=[onesf, c_], wr=[pB])
                        kb.op("act", lambda e: e.activation(out=c_[:, 12:16], in_=pA[:, 0:4], func=AF.Copy), rd=[pA],
                              wr=[c_])
                        kb.op("act", lambda e: e.activation(out=c_[:, 20:24], in_=pB[:, 0:4], func=AF.Copy), rd=[pB],
                              wr=[c_])
                        kb.op("act", lambda e: e.activation(out=c_[:, 16:20], in_=c_[:, 12:16], func=AF.Exp), rd=[c_],
                              wr=[c_])
                        kb.op("act", lambda e: e.activation(out=c_[:, 24:28], in_=c_[:, 20:24], func=AF.Exp), rd=[c_],
                              wr=[c_])
                        kb.op("dve", lambda e: e.tensor_tensor(out=c_[:, 28:32], in0=c_[:, 20:24], in1=c_[:, 12:16],
                                                               op=ALU.subtract), rd=[c_], wr=[c_])
                        kb.op("act", lambda e: e.activation(out=c_[:, 28:32], in_=c_[:, 28:32], func=AF.Exp), rd=[c_],
                              wr=[c_])
                        for h in range(4):
                            kb.op("dve", lambda e: e.tensor_scalar(out=dgb[:, h * 128:(h + 1) * 128], in0=ident[:],
                                                                   scalar1=c_[:, 12 + h:13 + h], scalar2=None,
                                                                   op0=ALU.mult), rd=[ident, c_], wr=[dgb])
                        kb.op("pe", lambda e: e.matmul(pA[:], lhsT=onesf[:], rhs=dgb[:], start=True, stop=True),
                              rd=[onesf, dgb], wr=[pA])
                        for h in range(4):
                            kb.op("dve", lambda e: e.tensor_scalar(out=tE[:, h * 128:(h + 1) * 128],
                                                                   in0=pA[:, h * 128:(h + 1) * 128],
                                                                   scalar1=c_[:, 12 + h:13 + h], scalar2=None,
                                                                   op0=ALU.subtract), rd=[pA, c_], wr=[tE])
                        kb.op("pool", lambda e: e.tensor_tensor(out=tE[:], in0=tE[:], in1=mk[2][:], op=ALU.add),
                              rd=[tE, mk[2]], wr=[tE])
                        kb.op("act", lambda e: e.activation(out=tE[:], in_=tE[:], func=AF.Exp), rd=[tE], wr=[tE])
                        for h in range(4):
                            kb.op("dve", lambda e: e.tensor_scalar(out=dgb[:, h * 128:(h + 1) * 128], in0=ident[:],
                                                                   scalar1=c_[:, h:h + 1], scalar2=None,
                                                                   op0=ALU.mult), rd=[ident, c_], wr=[dgb])
                        kb.op("pe", lambda e: e.matmul(pB[:], lhsT=onesf[:], rhs=dgb[:], start=True, stop=True),
                              rd=[onesf, dgb], wr=[pB])
                        for h in range(4):
                            kb.op("pe", lambda e: e.matmul(pC[:, h * 128:(h + 1) * 128], lhsT=kbT[h][:, tok],
                                                           rhs=kbT[h][:, tok], start=True, stop=True),
                                  rd=[kbT[h]], wr=[pC])
                            kb.op("pe", lambda e: e.matmul(pD[:, h * 128:(h + 1) * 128], lhsT=kbT[h][:, tok],
                                                           rhs=qbT[h][:, tok], start=True, stop=True),
                                  rd=[kbT[h], qbT[h]], wr=[pD])
                        kb.op("dve", lambda e: e.tensor_tensor(out=QKD[:], in0=pD[:], in1=tE[:], op=ALU.mult),
                              rd=[pD, tE], wr=[QKD])
                        U0, L0 = Um[0], Lm[0]
                        kb.op("dve", lambda e: e.tensor_tensor(out=U0[:], in0=pC[:], in1=tE[:], op=ALU.mult),
                              rd=[pC, tE], wr=[U0])
                        kb.op("dve", lambda e: e.scalar_tensor_tensor(out=U0[:], in0=U0[:], scalar=-1.0, in1=pB[:],
                                                                      op0=ALU.mult, op1=ALU.mult),
                              rd=[U0, pB], wr=[U0])
                        kb.op("pool", lambda e: e.tensor_tensor(out=U0[:], in0=U0[:], in1=mk[3][:], op=ALU.mult),
                              rd=[U0, mk[3]], wr=[U0])
                        for h in range(4):
                            kb.op("pe", lambda e: e.transpose(out=pI[0][:, h * 128:(h + 1) * 128],
                                                              in_=U0[:, h * 128:(h + 1) * 128], identity=ident[:]),
                                  rd=[U0, ident], wr=[pI[0]])
                        kb.op("act", lambda e: e.activation(out=L0[:], in_=pI[0][:], func=AF.Copy), rd=[pI[0]],
                              wr=[L0])
                        for h in range(4):
                            kb.op("pool", lambda e: e.tensor_tensor(out=Rm[:, h * 128:(h + 1) * 128],
                                                                    in0=U0[:, h * 128:(h + 1) * 128], in1=ident[:],
                                                                    op=ALU.add), rd=[U0, ident], wr=[Rm])
                        cu, cl = 0, 0
                        for m in range(1, 7):
                            Up, Lp = Um[cu], Lm[cl]
                            Un, Ln = Um[1 - cu], Lm[1 - cl]
                            if m < 6:
                                for h in range(4):
                                    hs = slice(h * 128, (h + 1) * 128)
                                    kb.op("pe", lambda e: e.matmul(pI[1][:, hs], lhsT=Lp[:, hs], rhs=Up[:, hs],
                                                                   start=True, stop=True), rd=[Lp, Up], wr=[pI[1]])
                            for h in range(4):
                                hs = slice(h * 128, (h + 1) * 128)
                                kb.op("pe", lambda e: e.matmul(pI[2][:, hs], lhsT=Up[:, hs], rhs=Lp[:, hs],
                                                               start=True, stop=True), rd=[Lp, Up], wr=[pI[2]])
                            if m < 6:
                                kb.op("act", lambda e: e.activation(out=Un[:], in_=pI[1][:], func=AF.Copy),
                                      rd=[pI[1]], wr=[Un])
                            kb.op("dve", lambda e: e.tensor_copy(out=Ln[:], in_=pI[2][:]), rd=[pI[2]], wr=[Ln])
                            for h in range(4):
                                hs = slice(h * 128, (h + 1) * 128)
                                kb.op("pe", lambda e: e.matmul(pI[3][:, hs], lhsT=Ln[:, hs], rhs=Rm[:, hs],
                                                               start=True, stop=True), rd=[Ln, Rm], wr=[pI[3]])
                            if m < 6:
                                kb.op("dve", lambda e: e.tensor_tensor(out=Rm[:], in0=Rm[:], in1=pI[3][:],
                                                                       op=ALU.add), rd=[Rm, pI[3]], wr=[Rm])
                            else:
                                kb.op("dve", lambda e: e.tensor_tensor(out=TT[:], in0=Rm[:], in1=pI[3][:],
                                                                       op=ALU.add), rd=[Rm, pI[3]], wr=[TT])
                            cu, cl = 1 - cu, 1 - cl
                        for h in range(4):
                            hs = slice(h * 128, (h + 1) * 128)
                            kb.op("pe", lambda e: e.matmul(pA[:, hs], lhsT=kbT[h][:, tok], rhs=identb[:], start=True,
                                                           stop=True), rd=[kbT[h], identb], wr=[pA])
                            kb.op("pe", lambda e: e.matmul(pB[:, hs], lhsT=vbT[h][:, tok], rhs=identb[:], start=True,
                                                           stop=True), rd=[vbT[h], identb], wr=[pB])
                        kb.op("act", lambda e: e.activation(out=kTM[:], in_=pA[:], func=AF.Copy), rd=[pA], wr=[kTM])
                        kb.op("act", lambda e: e.activation(out=vTM[:], in_=pB[:], func=AF.Copy), rd=[pB], wr=[vTM])
                        for h in range(4):
                            hs = slice(h * 128, (h + 1) * 128)
                            kb.op("pe", lambda e: e.matmul(pC[:, hs], lhsT=kbT[h][:, tok], rhs=Sb[:, hs], start=True,
                                                           stop=True), rd=[kbT[h], Sb], wr=[pC])
                        for h in range(4):
                            hs = slice(h * 128, (h + 1) * 128)
                            kb.op("dve", lambda e: e.scalar_tensor_tensor(
                                out=P2sb[:, hs], in0=pC[:, hs], scalar=c_[:, 16 + h:17 + h], in1=vTM[:, hs],
                                op0=ALU.mult, op1=ALU.subtract), rd=[pC, c_, vTM], wr=[P2sb])
                            kb.op("dve", lambda e: e.tensor_scalar(out=rhsv[:, hs], in0=P2sb[:, hs],
                                                                   scalar1=c_[:, 4 + h:5 + h], scalar2=None,
                                                                   op0=ALU.mult), rd=[P2sb, c_], wr=[rhsv])
                        for h in range(4):
                            hs = slice(h * 128, (h + 1) * 128)
                            kb.op("pe", lambda e: e.matmul(pD[:, hs], lhsT=TT[:, hs], rhs=rhsv[:, hs], start=True,
                                                           stop=True), rd=[TT, rhsv], wr=[pD])
                        kb.op("act", lambda e: e.activation(out=vn[:], in_=pD[:], func=AF.Copy), rd=[pD], wr=[vn])
                        for h in range(4):
                            hs = slice(h * 128, (h + 1) * 128)
                            kb.op("act", lambda e: e.activation(out=vnd[:, hs], in_=pD[:, hs], func=AF.Identity,
                                                                scale=c_[:, 28 + h:29 + h]), rd=[pD, c_], wr=[vnd])
                        for h in range(4):
                            hs = slice(h * 128, (h + 1) * 128)
                            kb.op("pe", lambda e: e.matmul(pA[:, hs], lhsT=qbT[h][:, tok], rhs=Sb[:, hs], start=True,
                                                           stop=True), rd=[qbT[h], Sb], wr=[pA])
                            kb.op("pe", lambda e: e.matmul(pB[:, hs], lhsT=QKD[:, hs], rhs=vn[:, hs], start=True,
                                                           stop=True), rd=[QKD, vn], wr=[pB])
                        kb.op("act", lambda e: e.activation(out=P2sb[:], in_=pB[:], func=AF.Copy), rd=[pB], wr=[P2sb])
                        for h in range(4):
                            hs = slice(h * 128, (h + 1) * 128)
                            kb.op("dve", lambda e: e.scalar_tensor_tensor(
                                out=ob[:, hs], in0=pA[:, hs], scalar=c_[:, 16 + h:17 + h], in1=P2sb[:, hs],
                                op0=ALU.mult, op1=ALU.add), rd=[pA, c_, P2sb], wr=[ob])
                        for h in range(4):
                            hs = slice(h * 128, (h + 1) * 128)
                            kb.op("pe", lambda e: e.matmul(pC[:, hs], lhsT=kTM[:, hs], rhs=vnd[:, hs], start=True,
                                                           stop=True), rd=[kTM, vnd], wr=[pC])
                        for h in range(4):
                            hs = slice(h * 128, (h + 1) * 128)
                            kb.op("dve", lambda e: e.scalar_tensor_tensor(
                                out=S[:, hs], in0=S[:, hs], scalar=c_[:, 24 + h:25 + h], in1=pC[:, hs],
                                op0=ALU.mult, op1=ALU.add), rd=[S, c_, pC], wr=[S])
                        kb.op("act", lambda e: e.activation(out=Sb[:], in_=S[:], func=AF.Copy), rd=[S], wr=[Sb])
                        for h in range(4):
                            hs = slice(h * 128, (h + 1) * 128)
                            kb.op("act", lambda e: e.activation(out=osq[:], in_=ob[:, hs], func=AF.Square,
                                                                accum_out=c_[:, 32 + h:33 + h]), rd=[ob],
                                  wr=[osq, c_])
                        kb.op("pool", lambda e: e.tensor_scalar(out=c_[:, 36:40], in0=c_[:, 32:36],
                                                                scalar1=1.0 / 128, scalar2=NORM_EPS, op0=ALU.mult,
                                                                op1=ALU.add), rd=[c_], wr=[c_])
                        kb.op("pool", lambda e: e.tensor_tensor(out=c_[:, 36:40], in0=c_[:, 36:40],
                                                                in1=mhalf4[:],
                                                                op=ALU.pow), rd=[c_, mhalf4], wr=[c_])
                        for h in range(4):
                            hs = slice(h * 128, (h + 1) * 128)
                            kb.op("dve", lambda e: e.tensor_scalar(out=ob[:, hs], in0=ob[:, hs],
                                                                   scalar1=c_[:, 36 + h:37 + h], scalar2=None,
                                                                   op0=ALU.mult), rd=[ob, c_], wr=[ob])
                            kb.op("pe", lambda e: e.transpose(out=pD[:, hs], in_=ob[:, hs], identity=ident[:]),
                                  rd=[ob, ident], wr=[pD])
                        for h in range(4):
                            hs = slice(h * 128, (h + 1) * 128)
                            kb.op("dve", lambda e: e.scalar_tensor_tensor(
                                out=mix[:, 4 + h, tok], in0=pD[:, hs], scalar=nw[:, 0:1], in1=zs[h][:, tok],
                                op0=ALU.mult, op1=ALU.mult), rd=[pD, nw, zs[h]], wr=[mix])
                    outproj_ln1(l, i, mix, wov, WO, x_src, Bx_src, nsub=2)
                kb.barrier()

        def sp_chain(c_, a, w, dst):
            y = c_[:, a:a + 4]
            t = c_[:, w:w + 4]
            d = c_[:, dst:dst + 4]

            def vop(fn):
                kb.op("dve", fn, rd=[c_], wr=[c_])

            vop(lambda e: e.tensor_scalar(out=t, in0=y, scalar1=-1.0, scalar2=None, op0=ALU.mult))
            vop(lambda e: e.tensor_tensor(out=t, in0=t, in1=y, op=ALU.min))
            kb.op("act", lambda e: e.activation(out=t, in_=t, func=AF.Exp), rd=[c_], wr=[c_])
            vop(lambda e: e.tensor_scalar(out=d, in0=t, scalar1=2.0, scalar2=None, op0=ALU.add))
            vop(lambda e: e.reciprocal(out=d, in_=d))
            vop(lambda e: e.tensor_tensor(out=t, in0=t, in1=d, op=ALU.mult))
            vop(lambda e: e.tensor_tensor(out=d, in0=t, in1=t, op=ALU.mult))
            p_ = c_[:, 16:20]
            vop(lambda e: e.tensor_scalar(out=p_, in0=d, scalar1=1.0 / 15, scalar2=1.0 / 13, op0=ALU.mult,
                                          op1=ALU.add))
            for cc in (11, 9, 7, 5, 3, 1):
                vop(lambda e: e.tensor_tensor(out=p_, in0=p_, in1=d, op=ALU.mult))
                vop(lambda e: e.tensor_scalar(out=p_, in0=p_, scalar1=1.0 / cc, scalar2=None, op0=ALU.add))
            vop(lambda e: e.tensor_tensor(out=p_, in0=p_, in1=t, op=ALU.mult))
            vop(lambda e: e.tensor_scalar(out=d, in0=y, scalar1=0.0, scalar2=None, op0=ALU.max))
            vop(lambda e: e.scalar_tensor_tensor(out=d, in0=p_, scalar=2.0, in1=d, op0=ALU.mult, op1=ALU.add))

        pass_t0()
        jh = jr = 0
        for l, kind in enumerate(kinds):
            x_src, Bx = (x_in, None) if l == 0 else (xres_d, B_xres)
            if kind == "rec":
                pass_rec(l, jr, x_src, Bx)
                jr += 1
            else:
                pass_hyb(l, jh, x_src, Bx)
                jh += 1
            pass_mlp(l, 0, False)
            pass_mlp(l, 1, l == L - 1)
        kb.finish()
        global LAST_CNT
        LAST_CNT = dict(kb.cnt)
        LAST_CNT['dma'] = {q: p['i'] for q, p in kb.dpool.items()}
    return nc


def prep_weights(inp, kinds, T):
    f32 = np.float32
    L = len(kinds)
    hyb_idx = [l // 2 for l, k in enumerate(kinds) if k == "hyb"]
    rec_idx = [l // 2 for l, k in enumerate(kinds) if k == "rec"]
    w = {}
    w["ident"] = np.eye(128, dtype=f32)
    ri = rec_idx or [0]
    w["rec_w_in"] = np.ascontiguousarray(inp["rec_w_in"][ri])
    w["rec_w_out"] = np.ascontiguousarray(inp["rec_w_out"][ri])
    w["rec_w_ax"] = np.ascontiguousarray(np.stack([inp["rec_w_a"][ri], inp["rec_w_x"][ri]], axis=1))
    vec = np.stack([inp["rec_conv_w"][ri][:, 0], inp["rec_conv_w"][ri][:, 1], inp["rec_conv_w"][ri][:, 2],
                    inp["rec_conv_w"][ri][:, 3], inp["rec_conv_b"][ri], inp["rec_b_a"][ri], inp["rec_b_x"][ri],
                    inp["rec_lambda"][ri]], axis=-1)
    w["rec_vec"] = np.ascontiguousarray(vec.reshape(len(ri), 8, 128, 8).transpose(0, 2, 1, 3))
    lay = list(range(L))
    gbs = np.stack([inp["ln1_g"][lay], inp["ln1_b"][lay], inp["ln2_g"][lay], inp["ln2_b"][lay]], axis=1)
    w["ln_gb"] = np.ascontiguousarray(np.broadcast_to(gbs[:, :, None, :], (L, 4, 128, D))).astype(f32)
    w["mlp_w1"] = np.ascontiguousarray(inp["mlp_w1"][lay])
    w["mlp_w2"] = np.ascontiguousarray(inp["mlp_w2"][lay])
    hi = hyb_idx or [0]
    qperm = np.concatenate([np.r_[jj * 64:(jj + 1) * 64, (4 + jj) * 64:(5 + jj) * 64] for jj in range(4)])
    sw64 = np.r_[32:64, 0:32]
    qswap = np.concatenate([h * 64 + sw64 for h in range(8)])[qperm]
    kswap = 512 + np.concatenate([h * 64 + sw64 for h in range(2)])
    cols = np.concatenate([qperm, np.arange(512, 640), qswap, kswap, np.arange(640, 768),
                           np.arange(768, 768 + 1536), np.arange(2304, 2816), np.arange(2816, 2824)])
    assert cols.shape[0] == HYBW
    w["hyb_w_in"] = np.ascontiguousarray(inp["hyb_w_in"][hi][:, :, cols])
    rows = np.concatenate([qperm, np.arange(512, 1024)])
    w["hyb_w_out"] = np.ascontiguousarray(inp["hyb_w_out"][hi][:, rows, :])
    cwv = inp["hyb_conv_w"][hi]
    w["hyb_convw"] = np.ascontiguousarray(cwv.reshape(len(hi), 4, 12, 128).transpose(0, 3, 2, 1))
    sm = np.concatenate([inp["hyb_sinks"][hi], inp["hyb_a_log"][hi], inp["hyb_dt_bias"][hi]], axis=1)
    w["hyb_small"] = np.ascontiguousarray(np.broadcast_to(sm[:, None, :], (len(hi), 128, 16))).astype(f32)
    w["hyb_normw"] = np.ascontiguousarray(inp["hyb_norm_w"][hi][:, :, None])
    half = 32
    inv_freq = (f32(10000.0) ** (-np.arange(half, dtype=f32) / f32(half))).astype(f32)
    ang = (np.arange(T, dtype=f32)[None, :] * inv_freq[:, None]).astype(f32)
    cosv, sinv = np.cos(ang).astype(f32), np.sin(ang).astype(f32)
    cos128 = np.concatenate([cosv, cosv, cosv, cosv], axis=0)
    sin128 = np.concatenate([-sinv, sinv, -sinv, sinv], axis=0)
    w["rope_cs"] = np.ascontiguousarray(np.stack([cos128, sin128], axis=0))
    jj_, ii_ = np.meshgrid(np.arange(128), np.arange(128), indexing="ij")
    m_c = (jj_ <= ii_).astype(f32)
    m_p = (jj_ > ii_).astype(f32)
    m_neg = np.where(ii_ >= jj_, 0.0, -30000.0).astype(f32)
    m_strict = (ii_ > jj_).astype(f32)
    w["masks"] = np.ascontiguousarray(np.stack([np.tile(m, (1, 4)) for m in (m_c, m_p, m_neg, m_strict)], axis=0))
    w["tri"] = np.ascontiguousarray((jj_ <= ii_).astype(f32))
    return w


_CACHE = {}


def kernel(**inputs):
    kinds = ["hyb", "rec", "hyb", "rec"]
    T = SEQ
    x = np.asarray(inputs["x"], dtype=np.float32)
    inp = {k: np.asarray(v, dtype=np.float32) for k, v in inputs.items()}
    w = prep_weights(inp, kinds, T)
    if "nc" not in _CACHE:
        _CACHE["nc"] = build_program(T, kinds)
    nc = _CACHE["nc"]
    in_maps = []
    for b in range(NB):
        m = dict(w)
        m["x"] = np.ascontiguousarray(x[b])
        in_maps.append(m)
    res = run_bass_kernel_spmd(nc, in_maps, core_ids=list(range(NB)))
    return np.stack([np.asarray(r["out"], dtype=np.float32) for r in res.results], axis=0)
```

```python
import math
from contextlib import ExitStack

import numpy as np
import concourse.bass as bass
import concourse.mybir as mybir
from concourse.bass_utils import run_bass_kernel_spmd

F32 = mybir.dt.float32
BF16 = mybir.dt.bfloat16
AF = mybir.ActivationFunctionType
ALU = mybir.AluOpType

D = 1024
DFF = 4096
DEPTH = 4
SEQ = 4096
NB = 8
DN_ALPHA = (2 * DEPTH) ** 0.25
LN_EPS = 1e-5
NORM_EPS = 1e-6
HYBW = 3464
DEBUG_STAGE = 2
LAST_CNT = None


class Buf:
    __slots__ = ("t", "w", "r", "name")

    def __init__(self, t=None, name=""):
        self.t = t
        self.w = None
        self.r = {}
        self.name = name

    def __getitem__(self, idx):
        return self.t[idx]


class KB:
    def __init__(self, nc, es):
        self.nc = nc
        self.es = es
        self.engs = {"pe": nc.tensor, "act": nc.scalar, "dve": nc.vector, "pool": nc.gpsimd, "sp": nc.sync}
        self.sem = {}
        self.cnt = {}
        self.seen = {k: {} for k in self.engs}
        for k in self.engs:
            self.sem[k] = es.enter_context(nc.semaphore("s_" + k))
            self.cnt[k] = 0
        self.dpool = {}
        for q in ("sp", "pool"):
            sems = [es.enter_context(nc.semaphore("d_%s%d" % (q, i))) for i in range(24)]
            self.dpool[q] = {"sems": sems, "cum": [0] * len(sems), "i": 0}
        self.nbuf = 0
        self.pe_sems = {id(self.sem['pe'])}

    def sb(self, shape, dt, name=None, es=None):
        self.nbuf += 1
        t = (es or self.es).enter_context(self.nc.sbuf_tensor("%s_%d" % (name or "sb", self.nbuf), list(shape), dt))
        return Buf(t, name)

    def barrier(self):
        toks = [(self.sem[k], self.cnt[k]) for k in self.engs if self.cnt[k]]
        for q, p in self.dpool.items():
            for s_, c in zip(p["sems"], p["cum"]):
                if c:
                    toks.append((s_, c))
        for e in self.engs:
            for t in toks:
                if t[0] is self.sem[e]:
                    continue
                self._wait(e, t)

    def psum(self, shape, dt=F32, name=None):
        self.nbuf += 1
        t = self.es.enter_context(self.nc.psum_tensor("%s_%d" % (name or "ps", self.nbuf), list(shape), dt))
        return Buf(t, name)

    def dram(self, name, shape, dt, kind="Internal"):
        return self.nc.dram_tensor(name, list(shape), dt, kind=kind).ap()

    def _wait(self, e, tok):
        if tok is None:
            return
        sem, val = tok
        d = self.seen[e]
        key = id(sem)
        if d.get(key, 0) >= val:
            return
        self.engs[e].wait_ge(sem, val)
        d[key] = val

    def _deps(self, e, rd, wr):
        for b in rd:
            if e == "pe" and b.w is not None and id(b.w[0]) in self.pe_sems:
                continue
            self._wait(e, b.w)
        for b in wr:
            if not (e == "pe" and b.w is not None and id(b.w[0]) in self.pe_sems):
                self._wait(e, b.w)
            for k, t in b.r.items():
                if e == "pe" and k == "pe":
                    continue
                self._wait(e, t)

    def _commit(self, e, tok, rd, wr):
        for b in rd:
            b.r[e] = tok
        for b in wr:
            b.w = tok
            b.r = {}

    def op(self, e, fn, rd=(), wr=()):
        if self.cnt[e] >= 24000:
            self.nrot = getattr(self, "nrot", 0) + 1
            self.sem[e] = self.es.enter_context(self.nc.semaphore("s_%s_r%d" % (e, self.nrot)))
            self.cnt[e] = 0
            if e == "pe":
                self.pe_sems.add(id(self.sem[e]))
        self._deps(e, rd, wr)
        inst = fn(self.engs[e])
        self.cnt[e] += 1
        inst.then_inc(self.sem[e], 1)
        tok = (self.sem[e], self.cnt[e])
        self._commit(e, tok, rd, wr)
        return tok

    def dma(self, q, out, in_, rd=(), wr=(), **kw):
        self._deps(q, rd, wr)
        p = self.dpool[q]
        i = p["i"] % len(p["sems"])
        p["i"] += 1
        s = p["sems"][i]
        self._wait(q, (s, p["cum"][i]) if p["cum"][i] else None)
        inst = self.engs[q].dma_start(out=out, in_=in_, **kw)
        inst.then_inc(s, 16)
        p["cum"][i] += 16
        tok = (s, p["cum"][i])
        for b in rd:
            b.r["dma_" + q + str(i)] = tok
        for b in wr:
            b.w = tok
            b.r = {}
        return tok

    def finish(self):
        for q, p in self.dpool.items():
            for s, c in zip(p["sems"], p["cum"]):
                if c:
                    self._wait("sp", (s, c))


def build_program(T, kinds, debug=None):
    nc = bass.Bass("TRN2", target_bir_lowering=False)
    L = len(kinds)
    n_h = max(1, sum(1 for k in kinds if k == "hyb"))
    n_r = max(1, sum(1 for k in kinds if k == "rec"))
    NT = T // 512
    NS = T // 128

    def din(name, shape, dt=F32):
        return nc.dram_tensor(name, list(shape), dt, kind="ExternalInput").ap()

    x_in = din("x", [T, D])
    ident_in = din("ident", [128, 128])
    hyb_w_in = din("hyb_w_in", [n_h, D, HYBW])
    hyb_w_out = din("hyb_w_out", [n_h, D, D])
    hyb_convw = din("hyb_convw", [n_h, 128, 12, 4])
    hyb_small = din("hyb_small", [n_h, 128, 16])
    hyb_normw = din("hyb_normw", [n_h, 128, 1])
    rope_cs = din("rope_cs", [2, 128, T])
    masks_in = din("masks", [4, 128, 512])
    tri_in = din("tri", [128, 128])
    rec_w_in = din("rec_w_in", [n_r, D, 2 * D])
    rec_w_out = din("rec_w_out", [n_r, D, D])
    rec_w_ax = din("rec_w_ax", [n_r, 2, 4, 256, 256])
    rec_vec = din("rec_vec", [n_r, 128, 8, 8])
    ln_gb = din("ln_gb", [L, 4, 128, D])
    mlp_w1 = din("mlp_w1", [L, D, DFF])
    mlp_w2 = din("mlp_w2", [L, DFF, D])
    out = nc.dram_tensor("out", [T, D], F32, kind="ExternalOutput").ap()

    es = ExitStack()
    with es:
        kb = KB(nc, es)
        xT_d = kb.dram("xT_d", [8, 128, T], BF16)
        x1T_d = kb.dram("x1T_d", [8, 128, T], BF16)
        xres_d = kb.dram("xres_d", [T, D], F32)
        x1_d = kb.dram("x1_d", [T, D], F32)
        p_d = kb.dram("p_d", [T, D], F32)
        B_xT = [Buf(name="xT%d" % i) for i in range(NT)]
        B_x1T = [Buf(name="x1T%d" % i) for i in range(NT)]
        B_xres = [Buf(name="xres%d" % i) for i in range(NS)]
        B_x1 = [Buf(name="x1_%d" % i) for i in range(NS)]
        B_p = [Buf(name="p%d" % i) for i in range(NS)]
        B_out = [Buf(name="out%d" % i) for i in range(NS)]

        ident = kb.sb([128, 128], F32, "ident")
        kb.dma("sp", ident[:], ident_in[:, :], wr=[ident])
        identb = kb.sb([128, 128], BF16, "identb")
        kb.op("act", lambda e: e.activation(out=identb[:], in_=ident[:], func=AF.Copy), rd=[ident], wr=[identb])

        ps = [kb.psum([128, 512], F32, "ps%d" % i) for i in range(8)]

        gb = [kb.sb([128, D], F32, "gb%d" % i) for i in range(2)]
        tm_in = [kb.sb([128, D], F32, "tmin%d" % i) for i in range(2)]
        r_sb = [kb.sb([128, D], F32, "r%d" % i) for i in range(2)]
        xo_sb = [kb.sb([128, D], F32, "xo%d" % i) for i in range(2)]
        xoT = [kb.sb([128, 8, 512], BF16, "xoT%d" % i) for i in range(1)]
        st6 = [kb.sb([128, 12], F32, "st%d" % i) for i in range(2)]
        mv = [kb.sb([128, 8], F32, "mv%d" % i) for i in range(2)]
        mhalf = kb.sb([128, 1], F32, "mhalf")
        kb.op("pool", lambda e: e.memset(mhalf[:], -0.5), wr=[mhalf])
        mhalf4 = kb.sb([128, 4], F32, "mhalf4")
        kb.op("pool", lambda e: e.memset(mhalf4[:], -0.5), wr=[mhalf4])
        cnt = {"ln": 0, "tm": 0}

        def wchunk(wreg):
            return Buf(wreg.t, "wchunk")

        def load_w(dst_ap, src_ap, wb):
            kb.dma("pool", dst_ap, src_ap, wr=[wb])

        def layer_norm(r, gbuf, bbuf):
            i = cnt["ln"] % 2
            cnt["ln"] += 1
            s6, m, xo = st6[i], mv[i], xo_sb[i]
            kb.op("dve", lambda e: e.bn_stats(out=s6[:, 0:6], in_=r[:, 0:512]), rd=[r], wr=[s6])
            kb.op("dve", lambda e: e.bn_stats(out=s6[:, 6:12], in_=r[:, 512:1024]), rd=[r], wr=[s6])
            kb.op("dve", lambda e: e.bn_aggr(out=m[:, 0:2], in_=s6[:, 0:12]), rd=[s6], wr=[m])
            kb.op("pool", lambda e: e.tensor_scalar(out=m[:, 2:3], in0=m[:, 1:2], scalar1=LN_EPS, scalar2=None,
                                                    op0=ALU.add), rd=[m], wr=[m])
            kb.op("pool", lambda e: e.tensor_tensor(out=m[:, 3:4], in0=m[:, 2:3], in1=mhalf[:], op=ALU.pow),
                  rd=[m, mhalf], wr=[m])
            kb.op("dve", lambda e: e.scalar_tensor_tensor(out=m[:, 4:5], in0=m[:, 0:1], scalar=-1.0, in1=m[:, 3:4],
                                                          op0=ALU.mult, op1=ALU.mult), rd=[m], wr=[m])
            kb.op("act", lambda e: e.activation(out=r[:], in_=r[:], func=AF.Identity, bias=m[:, 4:5],
                                                scale=m[:, 3:4]), rd=[r, m], wr=[r])
            kb.op("pool", lambda e: e.tensor_tensor(out=r[:], in0=r[:], in1=gbuf[:], op=ALU.mult),
                  rd=[r, gbuf], wr=[r])
            kb.op("pool", lambda e: e.tensor_tensor(out=xo[:], in0=r[:], in1=bbuf[:], op=ALU.add),
                  rd=[r, bbuf], wr=[xo])
            return xo

        def transpose_to_fm(xo, dst, s, pbanks):
            for half in range(2):
                pb = pbanks[half]
                for j in range(4):
                    kc = half * 4 + j
                    kb.op("pe", lambda e: e.transpose(
                        out=pb[:, j * 128:(j + 1) * 128], in_=xo[:, kc * 128:(kc + 1) * 128], identity=ident[:]),
                        rd=[xo, ident], wr=[pb])
                o_ap = dst[:, half * 4:half * 4 + 4, s * 128:(s + 1) * 128]
                i_ap = pb[:, :].rearrange("p (j t) -> p j t", j=4)
                if half == 0:
                    kb.op("act", lambda e: e.activation(out=o_ap, in_=i_ap, func=AF.Copy), rd=[pb], wr=[dst])
                else:
                    kb.op("dve", lambda e: e.tensor_copy(out=o_ap, in_=i_ap), rd=[pb], wr=[dst])

        def pass_t0():
            for i in range(NT):
                dst = xoT[0]
                for s in range(4):
                    g = i * 4 + s
                    xin = tm_in[cnt["tm"] % 2]
                    cnt["tm"] += 1
                    kb.dma("sp", xin[:], x_in[g * 128:(g + 1) * 128, :], wr=[xin])
                    transpose_to_fm(xin, dst, s, (ps[0 + 2 * (s % 2)], ps[1 + 2 * (s % 2)]))
                kb.dma("sp", xT_d[:, :, i * 512:(i + 1) * 512].rearrange("k p t -> p k t"), dst[:], rd=[dst],
                       wr=[B_xT[i]])
            kb.barrier()

        def pass_mlp(l, half, last):
            with ExitStack() as pes:
                wreg = kb.sb([128, 32768], BF16, "wreg", es=pes)
                xtile = [kb.sb([128, 8, 512], BF16, "xtile%d" % i_, es=pes) for i_ in range(2)]
                hT = [kb.sb([128, 512], BF16, "hT%d" % f, es=pes) for f in range(16)]
                relu_t = [kb.sb([128, 512], F32, "relu%d" % f, es=pes) for f in range(2)]
                w1v = wreg[:, 0:8 * 2048].rearrange("p (k f) -> p k f", k=8)
                w2v = wreg[:, 8 * 2048:8 * 2048 + 16 * 1024].rearrange("p (f d) -> p f d", f=16)
                f0 = half * 2048
                W1 = [wchunk(wreg) for _ in range(8)]
                W2 = [wchunk(wreg) for _ in range(16)]
                for k in range(8):
                    load_w(w1v[:, k, :], mlp_w1[l, k * 128:(k + 1) * 128, f0:f0 + 2048], W1[k])
                for f in range(16):
                    load_w(w2v[:, f, :], mlp_w2[l, f0 + f * 128:f0 + (f + 1) * 128, :], W2[f])
                if half == 1:
                    kb.dma("sp", gb[0][:], ln_gb[l, 2], wr=[gb[0]])
                    kb.dma("sp", gb[1][:], ln_gb[l, 3], wr=[gb[1]])
                for i in range(NT):
                    xt = xtile[i % 2]
                    kb.dma("sp", xt[:], x1T_d[:, :, i * 512:(i + 1) * 512].rearrange("k p t -> p k t"),
                           rd=[B_x1T[i]], wr=[xt])
                    for f in range(16):
                        pb = ps[f % 2]
                        for k in range(8):
                            kb.op("pe", lambda e: e.matmul(
                                pb[:], lhsT=w1v[:, k, f * 128:(f + 1) * 128], rhs=xt[:, k, :],
                                start=(k == 0), stop=(k == 7)), rd=[W1[k], xt], wr=[pb])
                        rt = relu_t[f % 2]
                        kb.op("act", lambda e: e.activation(out=rt[:], in_=pb[:], func=AF.Relu), rd=[pb], wr=[rt])
                        kb.op("dve", lambda e: e.scalar_tensor_tensor(
                            out=hT[f][:], in0=pb[:], scalar=0.0, in1=rt[:], op0=ALU.max, op1=ALU.mult),
                            rd=[pb, rt], wr=[hT[f]])
                    dstT = xoT[0]
                    for s in range(4):
                        g = i * 4 + s
                        pbs = (ps[2 + 2 * (s % 2)], ps[3 + 2 * (s % 2)])
                        for hh in range(2):
                            for f in range(16):
                                kb.op("pe", lambda e: e.matmul(
                                    pbs[hh][:], lhsT=hT[f][:, s * 128:(s + 1) * 128],
                                    rhs=w2v[:, f, hh * 512:(hh + 1) * 512], start=(f == 0), stop=(f == 15)),
                                    rd=[W2[f], hT[f]], wr=[pbs[hh]])
                        xin = tm_in[cnt["tm"] % 2]
                        cnt["tm"] += 1
                        r = r_sb[g % 2]
                        if half == 0:
                            kb.dma("sp", xin[:], x1_d[g * 128:(g + 1) * 128, :], rd=[B_x1[g]], wr=[xin])
                            for hh in range(2):
                                kb.op("dve", lambda e: e.scalar_tensor_tensor(
                                    out=r[:, hh * 512:(hh + 1) * 512], in0=xin[:, hh * 512:(hh + 1) * 512],
                                    scalar=float(DN_ALPHA), in1=pbs[hh][:], op0=ALU.mult, op1=ALU.add),
                                    rd=[xin, pbs[hh]], wr=[r])
                            kb.dma("sp", p_d[g * 128:(g + 1) * 128, :], r[:], rd=[r], wr=[B_p[g]])
                        else:
                            kb.dma("sp", xin[:], p_d[g * 128:(g + 1) * 128, :], rd=[B_p[g]], wr=[xin])
                            for hh in range(2):
                                kb.op("dve", lambda e: e.tensor_tensor(
                                    out=r[:, hh * 512:(hh + 1) * 512], in0=xin[:, hh * 512:(hh + 1) * 512],
                                    in1=pbs[hh][:], op=ALU.add), rd=[xin, pbs[hh]], wr=[r])
                            xo = layer_norm(r, gb[0], gb[1])
                            if last:
                                kb.dma("sp", out[g * 128:(g + 1) * 128, :], xo[:], rd=[xo], wr=[B_out[g]])
                            else:
                                kb.dma("sp", xres_d[g * 128:(g + 1) * 128, :], xo[:], rd=[xo], wr=[B_xres[g]])
                                transpose_to_fm(xo, dstT, s, (ps[6], ps[7]))
                    if half == 1 and not last:
                        kb.dma("sp", xT_d[:, :, i * 512:(i + 1) * 512].rearrange("k p t -> p k t"), dstT[:],
                               rd=[dstT], wr=[B_xT[i]])
                kb.barrier()

        def outproj_ln1(l, i, mix, wov, WO, x_src, Bx_src, nsub=4):
            dstT = xoT[0]
            TW = nsub * 128
            for s in range(nsub):
                g = i * nsub + s
                pbs = (ps[4], ps[5])
                for hh in range(2):
                    for k in range(8):
                        kb.op("pe", lambda e: e.matmul(
                            pbs[hh][:], lhsT=mix[:, k, s * 128:(s + 1) * 128],
                            rhs=wov[:, k, hh * 512:(hh + 1) * 512], start=(k == 0), stop=(k == 7)),
                            rd=[WO[k], mix], wr=[pbs[hh]])
                xin = tm_in[cnt["tm"] % 2]
                cnt["tm"] += 1
                if Bx_src is None:
                    kb.dma("sp", xin[:], x_src[g * 128:(g + 1) * 128, :], wr=[xin])
                else:
                    kb.dma("sp", xin[:], x_src[g * 128:(g + 1) * 128, :], rd=[Bx_src[g]], wr=[xin])
                r = r_sb[g % 2]
                for hh in range(2):
                    kb.op("dve", lambda e: e.scalar_tensor_tensor(
                        out=r[:, hh * 512:(hh + 1) * 512], in0=xin[:, hh * 512:(hh + 1) * 512],
                        scalar=float(DN_ALPHA), in1=pbs[hh][:], op0=ALU.mult, op1=ALU.add),
                        rd=[xin, pbs[hh]], wr=[r])
                yield
                xo = layer_norm(r, gb[0], gb[1])
                kb.dma("sp", x1_d[g * 128:(g + 1) * 128, :], xo[:], rd=[xo], wr=[B_x1[g]])
                yield
                transpose_to_fm(xo, dstT, s, (ps[6], ps[7]))
                yield
            if nsub == 4:
                kb.dma("sp", x1T_d[:, :, i * 512:(i + 1) * 512].rearrange("k p t -> p k t"), dstT[:], rd=[dstT],
                       wr=[B_x1T[i]])
            else:
                kb.dma("sp", x1T_d[:, :, i * TW:(i + 1) * TW].rearrange("k p t -> p k t"), dstT[:, :, 0:TW],
                       rd=[dstT], wr=[B_x1T[(i * TW) // 512]])

        def pass_rec(l, j, x_src, Bx_src):
            with ExitStack() as pes:
                wreg = kb.sb([128, 28672], BF16, "wreg", es=pes)
                xtile = [kb.sb([128, 8, 512], BF16, "xtile%d" % i_, es=pes) for i_ in range(2)]
                mix = kb.sb([128, 8, 512], BF16, "mix", es=pes)
                rvec = kb.sb([128, 8, 8], F32, "rvec", es=pes)
                rtmp = kb.sb([128, 8, 8], F32, "rtmp", es=pes)
                c8 = kb.sb([128, 8], F32, "c8", es=pes)
                hcar = kb.sb([128, 8], F32, "hcar", es=pes)
                cbuf = [kb.sb([128, 515], F32, "cbuf%d" % k, es=pes) for k in range(8)]
                xcf = [kb.sb([128, 512], F32, "xcf%d" % k, es=pes) for k in range(8)]
                xcb = [kb.sb([128, 512], BF16, "xcb%d" % k, es=pes) for k in range(8)]
                gl = [kb.sb([128, 512], F32, "gl%d" % k, es=pes) for k in range(2)]
                g_r = [kb.sb([128, 512], F32, "g_r%d" % k, es=pes) for k in range(2)]
                g_i = [kb.sb([128, 512], F32, "g_i%d" % k, es=pes) for k in range(2)]
                g_a = [kb.sb([128, 512], F32, "g_a%d" % k, es=pes) for k in range(2)]
                g_b = [kb.sb([128, 512], F32, "g_b%d" % k, es=pes) for k in range(2)]
                winv = wreg[:, 0:8 * 2048].rearrange("p (k c) -> p k c", k=8)
                o1 = 8 * 2048
                wov = wreg[:, o1:o1 + 8 * 1024].rearrange("p (k c) -> p k c", k=8)
                o2 = o1 + 8 * 1024
                waxv = wreg[:, o2:o2 + 16 * 256].rearrange("p (a h k c) -> p a h k c", a=2, h=4, k=2)
                WI = [wchunk(wreg) for _ in range(8)]
                WO = [wchunk(wreg) for _ in range(8)]
                WAX = [wchunk(wreg) for _ in range(2)]
                for k in range(8):
                    load_w(winv[:, k, :], rec_w_in[j, k * 128:(k + 1) * 128, :], WI[k])
                for a in range(2):
                    load_w(waxv[:, a], rec_w_ax[j, a].rearrange("h (k p) c -> p h k c", p=128), WAX[a])
                for k in range(8):
                    load_w(wov[:, k, :], rec_w_out[j, k * 128:(k + 1) * 128, :], WO[k])
                kb.dma("sp", gb[0][:], ln_gb[l, 0], wr=[gb[0]])
                kb.dma("sp", gb[1][:], ln_gb[l, 1], wr=[gb[1]])
                kb.dma("sp", rvec[:], rec_vec[j], wr=[rvec])
                lam = rvec[:, :, 7]
                sp_ = rtmp
                V = "dve"

                def vop(fn):
                    kb.op(V, fn, rd=[rvec, rtmp], wr=[rtmp])

                vop(lambda e: e.tensor_scalar(out=sp_[:, 0, :], in0=lam, scalar1=-1.0, scalar2=None, op0=ALU.mult))
                vop(lambda e: e.tensor_tensor(out=sp_[:, 1, :], in0=sp_[:, 0, :], in1=lam, op=ALU.min))
                kb.op("act", lambda e: e.activation(out=sp_[:, 2, :], in_=sp_[:, 1, :], func=AF.Exp), rd=[rtmp],
                      wr=[rtmp])
                vop(lambda e: e.tensor_scalar(out=sp_[:, 3, :], in0=sp_[:, 2, :], scalar1=2.0, scalar2=None,
                                              op0=ALU.add))
                vop(lambda e: e.reciprocal(out=sp_[:, 3, :], in_=sp_[:, 3, :]))
                vop(lambda e: e.tensor_tensor(out=sp_[:, 3, :], in0=sp_[:, 3, :], in1=sp_[:, 2, :], op=ALU.mult))
                vop(lambda e: e.tensor_tensor(out=sp_[:, 4, :], in0=sp_[:, 3, :], in1=sp_[:, 3, :], op=ALU.mult))
                vop(lambda e: e.tensor_scalar(out=sp_[:, 5, :], in0=sp_[:, 4, :], scalar1=1.0 / 15,
                                              scalar2=1.0 / 13, op0=ALU.mult, op1=ALU.add))
                for c in (11, 9, 7, 5, 3, 1):
                    vop(lambda e: e.tensor_tensor(out=sp_[:, 5, :], in0=sp_[:, 5, :], in1=sp_[:, 4, :], op=ALU.mult))
                    vop(lambda e: e.tensor_scalar(out=sp_[:, 5, :], in0=sp_[:, 5, :], scalar1=1.0 / c, scalar2=None,
                                                  op0=ALU.add))
                vop(lambda e: e.tensor_tensor(out=sp_[:, 5, :], in0=sp_[:, 5, :], in1=sp_[:, 3, :], op=ALU.mult))
                vop(lambda e: e.tensor_scalar(out=sp_[:, 6, :], in0=sp_[:, 0, :], scalar1=0.0, scalar2=None,
                                              op0=ALU.max))
                vop(lambda e: e.scalar_tensor_tensor(out=sp_[:, 6, :], in0=sp_[:, 5, :], scalar=2.0,
                                                     in1=sp_[:, 6, :], op0=ALU.mult, op1=ALU.add))
                kb.op(V, lambda e: e.tensor_scalar(out=c8[:, :], in0=sp_[:, 6, :], scalar1=-8.0, scalar2=None,
                                                   op0=ALU.mult), rd=[rtmp], wr=[c8])
                for kc in range(8):
                    kb.op("pool", lambda e: e.memset(cbuf[kc][:, 0:3], 0.0), wr=[cbuf[kc]])
                kb.op("pool", lambda e: e.memset(hcar[:], 0.0), wr=[hcar])

                for i in range(NT):
                    xt = xtile[i % 2]
                    kb.dma("sp", xt[:], xT_d[:, :, i * 512:(i + 1) * 512].rearrange("k p t -> p k t"),
                           rd=[B_xT[i]], wr=[xt])

                    def inproj(cc, pb):
                        for k in range(8):
                            kb.op("pe", lambda e: e.matmul(
                                pb[:], lhsT=winv[:, k, cc * 128:(cc + 1) * 128], rhs=xt[:, k, :],
                                start=(k == 0), stop=(k == 7)), rd=[WI[k], xt], wr=[pb])

                    for cc in range(8):
                        pb = ps[cc % 2]
                        inproj(cc, pb)
                        kb.op("act", lambda e: e.activation(out=cbuf[cc][:, 3:515], in_=pb[:], func=AF.Copy),
                              rd=[pb], wr=[cbuf[cc]])
                    for kc in range(8):
                        cb = cbuf[kc]
                        xc = xcf[kc]
                        kb.op("dve", lambda e: e.tensor_scalar(
                            out=xc[:], in0=cb[:, 3:515], scalar1=rvec[:, kc, 3:4], scalar2=rvec[:, kc, 4:5],
                            op0=ALU.mult, op1=ALU.add), rd=[cb, rvec], wr=[xc])
                        for jj in range(3):
                            kb.op("dve", lambda e: e.scalar_tensor_tensor(
                                out=xc[:], in0=cb[:, jj:jj + 512], scalar=rvec[:, kc, jj:jj + 1], in1=xc[:],
                                op0=ALU.mult, op1=ALU.add), rd=[cb, rvec, xc], wr=[xc])
                        kb.op("pool", lambda e: e.tensor_copy(out=cb[:, 0:3], in_=cb[:, 512:515]), rd=[cb], wr=[cb])
                        kb.op("act", lambda e: e.activation(out=xcb[kc][:], in_=xc[:], func=AF.Copy), rd=[xc],
                              wr=[xcb[kc]])
                    for oc in range(8):
                        h = oc // 2
                        cl = (oc % 2) * 128
                        pa, px, pg = ps[2], ps[3], ps[oc % 2]
                        for a, pb in ((0, pa), (1, px)):
                            for k in range(2):
                                kb.op("pe", lambda e: e.matmul(
                                    pb[:], lhsT=waxv[:, a, h, k, cl:cl + 128], rhs=xcb[2 * h + k][:],
                                    start=(k == 0), stop=(k == 1)), rd=[WAX[a], xcb[2 * h + k]], wr=[pb])
                        inproj(8 + oc, pg)
                        rg, ig, av, bv, gg = g_r[oc % 2], g_i[oc % 2], g_a[oc % 2], g_b[oc % 2], gl[oc % 2]
                        kb.op("act", lambda e: e.activation(out=rg[:], in_=pa[:], func=AF.Sigmoid,
                                                            bias=rvec[:, oc, 5:6]), rd=[pa, rvec], wr=[rg])
                        kb.op("act", lambda e: e.activation(out=ig[:], in_=px[:], func=AF.Sigmoid,
                                                            bias=rvec[:, oc, 6:7]), rd=[px, rvec], wr=[ig])
                        kb.op("act", lambda e: e.activation(out=gg[:], in_=pg[:], func=AF.Gelu), rd=[pg], wr=[gg])
                        kb.op("act", lambda e: e.activation(out=av[:], in_=rg[:], func=AF.Exp,
                                                            scale=c8[:, oc:oc + 1]), rd=[rg, c8], wr=[av])
                        kb.op("pool", lambda e: e.tensor_tensor(out=rg[:], in0=av[:], in1=av[:], op=ALU.mult),
                              rd=[av], wr=[rg])
                        kb.op("pool", lambda e: e.tensor_scalar(out=rg[:], in0=rg[:], scalar1=-1.0, scalar2=1.0,
                                                                op0=ALU.mult, op1=ALU.add), rd=[rg], wr=[rg])
                        kb.op("act", lambda e: e.activation(out=rg[:], in_=rg[:], func=AF.Sqrt), rd=[rg], wr=[rg])
                        kb.op("dve", lambda e: e.tensor_tensor(out=bv[:], in0=ig[:], in1=xcf[oc][:], op=ALU.mult),
                              rd=[ig, xcf[oc]], wr=[bv])
                        kb.op("dve", lambda e: e.tensor_tensor(out=bv[:], in0=bv[:], in1=rg[:], op=ALU.mult),
                              rd=[bv, rg], wr=[bv])
                        kb.op("dve", lambda e: e.tensor_tensor_scan(out=ig[:], data0=av[:], data1=bv[:],
                                                                    initial=hcar[:, oc:oc + 1], op0=ALU.mult,
                                                                    op1=ALU.add), rd=[av, bv, hcar], wr=[ig])
                        kb.op("pool", lambda e: e.tensor_copy(out=hcar[:, oc:oc + 1], in_=ig[:, 511:512]), rd=[ig],
                              wr=[hcar])
                        kb.op("pool", lambda e: e.tensor_tensor(out=mix[:, oc, :], in0=ig[:], in1=gg[:],
                                                                op=ALU.mult), rd=[ig, gg], wr=[mix])
                    for _ in outproj_ln1(l, i, mix, wov, WO, x_src, Bx_src):
                        pass
                kb.barrier()

        def pass_hyb(l, j, x_src, Bx_src):
            with ExitStack() as pes:
                TW = 256
                NTH = T // TW

                def lb(shape, dt, name):
                    return kb.sb(shape, dt, name, es=pes)

                wreg = lb([128, 8 * HYBW + 8 * 1024], BF16, "wreg")
                winv = wreg[:, 0:8 * HYBW].rearrange("p (k c) -> p k c", k=8)
                wov = wreg[:, 8 * HYBW:].rearrange("p (k c) -> p k c", k=8)
                WI = [wchunk(wreg) for _ in range(8)]
                WO = [wchunk(wreg) for _ in range(8)]
                hw = HYBW // 2
                for k in range(8):
                    load_w(winv[:, k, 0:hw], hyb_w_in[j, k * 128:(k + 1) * 128, 0:hw], WI[k])
                    load_w(winv[:, k, hw:HYBW], hyb_w_in[j, k * 128:(k + 1) * 128, hw:HYBW], WI[k])
                for k in range(8):
                    load_w(wov[:, k, :], hyb_w_out[j, k * 128:(k + 1) * 128, :], WO[k])
                kb.dma("sp", gb[0][:], ln_gb[l, 0], wr=[gb[0]])
                kb.dma("sp", gb[1][:], ln_gb[l, 1], wr=[gb[1]])
                mk2_ = lb([128, 512], F32, "mk2")
                mk3_ = lb([128, 512], F32, "mk3")
                mk = [None, None, mk2_, mk3_]
                maskcb = lb([128, 512], BF16, "maskcb")
                maskpb = lb([128, 512], BF16, "maskpb")
                kb.dma("sp", mk2_[:], masks_in[0], wr=[mk2_])
                kb.dma("sp", mk3_[:], masks_in[1], wr=[mk3_])
                kb.op("act", lambda e: e.activation(out=maskcb[:], in_=mk2_[:], func=AF.Copy), rd=[mk2_], wr=[maskcb])
                kb.op("act", lambda e: e.activation(out=maskpb[:], in_=mk3_[:], func=AF.Copy), rd=[mk3_], wr=[maskpb])
                kb.dma("sp", mk2_[:], masks_in[2], rd=[], wr=[mk2_])
                kb.dma("sp", mk3_[:], masks_in[3], rd=[], wr=[mk3_])
                tri = lb([128, 128], F32, "tri")
                kb.dma("sp", tri[:], tri_in[:, :], wr=[tri])
                onesf = lb([128, 128], F32, "onesf")
                onesb = lb([128, 128], BF16, "onesb")
                kb.op("pool", lambda e: e.memset(onesf[:], 1.0), wr=[onesf])
                kb.op("pool", lambda e: e.memset(onesb[:], 1.0), wr=[onesb])
                small = lb([128, 16], F32, "small")
                kb.dma("sp", small[:], hyb_small[j], wr=[small])
                esink = lb([128, 8], F32, "esink")
                nea = lb([128, 4], F32, "nea")
                kb.op("act", lambda e: e.activation(out=esink[:], in_=small[:, 0:8], func=AF.Exp), rd=[small], wr=[esink])
                kb.op("act", lambda e: e.activation(out=nea[:], in_=small[:, 8:12], func=AF.Exp), rd=[small], wr=[nea])
                kb.op("dve", lambda e: e.tensor_scalar(out=nea[:], in0=nea[:], scalar1=-1.0, scalar2=None, op0=ALU.mult),
                      rd=[nea], wr=[nea])
                cw = lb([128, 12, 4], F32, "cw")
                kb.dma("sp", cw[:], hyb_convw[j], wr=[cw])
                nw = lb([128, 1], F32, "nw")
                kb.dma("sp", nw[:], hyb_normw[j], wr=[nw])
                qTt = lb([128, 4, TW], BF16, "qTt")
                kTt = lb([128, 128 + TW], BF16, "kTt")
                vtm = lb([128, 3, 128], BF16, "vtm")
                cb = [lb([128, 3 + TW], F32, "cb%d" % i) for i in range(2)]
                halo = lb([128, 12, 3], F32, "halo")
                kb.op("pool", lambda e: e.memset(halo[:], 0.0), wr=[halo])
                cvo = [lb([128, TW], F32, "cvo%d" % i) for i in range(2)]
                sqb = [lb([128, TW], BF16, "sqb%d" % i) for i in range(2)]
                rnb = [lb([128, TW], F32, "rnb%d" % i) for i in range(2)]
                xtile = [lb([128, 8, TW], BF16, "xtile%d" % i_) for i_ in range(2)]
                qbT2 = [[lb([128, TW], BF16, "qbT%d" % i_) for i_ in range(4)] for _ in range(2)]
                kbT2 = [[lb([128, TW], BF16, "kbT%d" % i_) for i_ in range(4)] for _ in range(2)]
                vbT2 = [[lb([128, TW], BF16, "vbT%d" % i_) for i_ in range(4)] for _ in range(2)]
                zs2 = [[lb([128, TW], F32, "zs%d" % i_) for i_ in range(4)] for _ in range(2)]
                mix2 = [lb([128, 8, TW], BF16, "mix%d" % i_) for i_ in range(2)]
                cs = [lb([128, 2, TW], F32, "cs%d" % i) for i in range(2)]
                rt1 = lb([128, TW], F32, "rt1")
                rt2 = lb([128, TW], F32, "rt2")
                Ec = lb([128, 512], BF16, "Ec")
                Ep = lb([128, 512], BF16, "Ep")
                rden = lb([128, 512], F32, "rden")
                sc2 = [[lb([128, 40], F32, "sc%d" % i_) for i_ in range(2)] for _ in range(2)]
                dgb = lb([128, 512], F32, "dgb")
                tE = lb([128, 512], F32, "tE")
                QKD = lb([128, 512], BF16, "QKD")
                Um = [lb([128, 512], F32, "Um%d" % i) for i in range(2)]
                Lm = [lb([128, 512], F32, "Lm%d" % i) for i in range(2)]
                Rm = lb([128, 512], F32, "Rm")
                TT = lb([128, 512], BF16, "TT")
                S = lb([128, 512], F32, "S")
                Sb = lb([128, 512], BF16, "Sb")
                kb.op("pool", lambda e: e.memset(S[:], 0.0), wr=[S])
                kb.op("pool", lambda e: e.memset(Sb[:], 0.0), wr=[Sb])
                kTM = lb([128, 512], BF16, "kTM")
                vTM = lb([128, 512], BF16, "vTM")
                rhsv = lb([128, 512], BF16, "rhsv")
                vn = lb([128, 512], BF16, "vn")
                vnd = lb([128, 512], BF16, "vnd")
                P2sb = lb([128, 512], F32, "P2sb")
                ob = lb([128, 512], F32, "ob")
                osq = lb([128, 128], F32, "osq")
                h4 = lambda t_: t_[:, :].rearrange("p (h c) -> p h c", h=4)

                def g_proj(i):
                    qbT, kbT, vbT, zs, sc = qbT2[i % 2], kbT2[i % 2], vbT2[i % 2], zs2[i % 2], sc2[i % 2]
                    xt = xtile[i % 2]
                    kb.dma("sp", xt[:, :, 0:TW], xT_d[:, :, i * TW:(i + 1) * TW].rearrange("k p t -> p k t"),
                           rd=[B_xT[(i * TW) // 512]], wr=[xt])
                    cst = cs[i % 2]
                    kb.dma("sp", cst[:], rope_cs[:, :, i * TW:(i + 1) * TW].rearrange("a p t -> p a t"), wr=[cst])

                    def proj(c0, ncol, pb, pcols=None):
                        for k in range(8):
                            kb.op("pe", lambda e: e.matmul(
                                pb[0:ncol, 0:TW], lhsT=winv[:, k, c0:c0 + ncol], rhs=xt[:, k, 0:TW],
                                start=(k == 0), stop=(k == 7)), rd=[WI[k], xt], wr=[pb])

                    for c in range(5):
                        yield
                        p1, p2 = ps[0], ps[1]
                        proj(c * 128, 128, p1)
                        proj(640 + c * 128, 128, p2)
                        kb.op("dve", lambda e: e.tensor_tensor(out=rt1[:], in0=p1[:, 0:TW], in1=cst[:, 0, :],
                                                               op=ALU.mult), rd=[p1, cst], wr=[rt1])
                        kb.op("dve", lambda e: e.tensor_tensor(out=rt2[:], in0=p2[:, 0:TW], in1=cst[:, 1, :],
                                                               op=ALU.mult), rd=[p2, cst], wr=[rt2])
                        if c < 4:
                            kb.op("pool", lambda e: e.tensor_tensor(out=qTt[:, c, :], in0=rt1[:], in1=rt2[:],
                                                                    op=ALU.add), rd=[rt1, rt2], wr=[qTt])
                        else:
                            kb.op("pool", lambda e: e.tensor_tensor(out=kTt[:, 128:128 + TW], in0=rt1[:], in1=rt2[:],
                                                                    op=ALU.add), rd=[rt1, rt2], wr=[kTt])
                    yield
                    pv = ps[0]
                    for s in range(2):
                        for k in range(8):
                            kb.op("pe", lambda e: e.matmul(
                                pv[:, s * 128:(s + 1) * 128], lhsT=xt[:, k, s * 128:(s + 1) * 128],
                                rhs=winv[:, k, 1280:1408], start=(k == 0), stop=(k == 7)), rd=[WI[k], xt], wr=[pv])
                    kb.op("act", lambda e: e.activation(out=vtm[:, 1:3, :], in_=pv[:, 0:256].rearrange(
                        "p (s c) -> p s c", s=2), func=AF.Copy), rd=[pv], wr=[vtm])
                    yield
                    pbd = ps[1]
                    for s in range(2):
                        for k in range(8):
                            kb.op("pe", lambda e: e.matmul(
                                pbd[:, s * 8:(s + 1) * 8], lhsT=xt[:, k, s * 128:(s + 1) * 128],
                                rhs=winv[:, k, 3456:3464], start=(k == 0), stop=(k == 7)), rd=[WI[k], xt], wr=[pbd])
                    for s in range(2):
                        yield
                        c_ = sc[s]
                        kb.op("act", lambda e: e.activation(out=c_[:, 0:4], in_=pbd[:, s * 8:s * 8 + 4],
                                                            func=AF.Sigmoid), rd=[pbd], wr=[c_])
                        kb.op("dve", lambda e: e.tensor_scalar(out=c_[:, 4:8], in0=c_[:, 0:4], scalar1=-1.0,
                                                               scalar2=None, op0=ALU.mult), rd=[c_], wr=[c_])
                        kb.op("dve", lambda e: e.tensor_tensor(out=c_[:, 32:36], in0=pbd[:, s * 8 + 4:s * 8 + 8],
                                                               in1=small[:, 12:16], op=ALU.add),
                              rd=[pbd, small], wr=[c_])
                        sp_chain(c_, 32, 36, 12)
                        kb.op("dve", lambda e: e.tensor_tensor(out=c_[:, 8:12], in0=c_[:, 12:16], in1=nea[:],
                                                               op=ALU.mult), rd=[c_, nea], wr=[c_])
                    for c in range(12):
                        yield
                        pb = ps[c % 2]
                        proj(1408 + c * 128, 128, pb)
                        cbt = cb[c % 2]
                        kb.op("pool", lambda e: e.tensor_copy(out=cbt[:, 0:3], in_=halo[:, c, :]), rd=[halo],
                              wr=[cbt])
                        kb.op("act", lambda e: e.activation(out=cbt[:, 3:3 + TW], in_=pb[:, 0:TW], func=AF.Copy),
                              rd=[pb], wr=[cbt])
                        co = cvo[c % 2]
                        kb.op("dve", lambda e: e.tensor_scalar(out=co[:], in0=cbt[:, 3:3 + TW], scalar1=cw[:, c, 3:4],
                                                               scalar2=None, op0=ALU.mult), rd=[cbt, cw], wr=[co])
                        for jj in range(3):
                            kb.op("dve", lambda e: e.scalar_tensor_tensor(
                                out=co[:], in0=cbt[:, jj:jj + TW], scalar=cw[:, c, jj:jj + 1], in1=co[:],
                                op0=ALU.mult, op1=ALU.add), rd=[cbt, cw, co], wr=[co])
                        kb.op("pool", lambda e: e.tensor_copy(out=halo[:, c, :], in_=cbt[:, TW:TW + 3]), rd=[cbt],
                              wr=[halo])
                        if c >= 8:
                            kb.op("act", lambda e: e.activation(out=vbT[c - 8][:], in_=co[:], func=AF.Silu),
                                  rd=[co], wr=[vbT[c - 8]])
                            continue
                        kb.op("act", lambda e: e.activation(out=co[:], in_=co[:], func=AF.Silu), rd=[co], wr=[co])
                        yield
                        sq = sqb[c % 2]
                        kb.op("act", lambda e: e.activation(out=sq[:], in_=co[:], func=AF.Square), rd=[co], wr=[sq])
                        pn = ps[2 + c % 2]
                        kb.op("pe", lambda e: e.matmul(pn[:, 0:TW], lhsT=onesb[:], rhs=sq[:], start=True, stop=True),
                              rd=[onesb, sq], wr=[pn])
                        rn = rnb[c % 2]
                        kb.op("dve", lambda e: e.tensor_scalar(out=rn[:], in0=pn[:, 0:TW], scalar1=NORM_EPS,
                                                               scalar2=None, op0=ALU.add), rd=[pn], wr=[rn])
                        kb.op("act", lambda e: e.activation(out=rn[:], in_=rn[:], func=AF.Ln), rd=[rn], wr=[rn])
                        kb.op("act", lambda e: e.activation(out=rn[:], in_=rn[:], func=AF.Exp, scale=-0.5),
                              rd=[rn], wr=[rn])
                        dstq = qbT[c] if c < 4 else kbT[c - 4]
                        if c < 4:
                            kb.op("dve", lambda e: e.scalar_tensor_tensor(
                                out=dstq[:], in0=co[:], scalar=float(128 ** -0.5), in1=rn[:], op0=ALU.mult,
                                op1=ALU.mult), rd=[co, rn], wr=[dstq])
                        else:
                            kb.op("dve", lambda e: e.tensor_tensor(out=dstq[:], in0=co[:], in1=rn[:], op=ALU.mult),
                                  rd=[co, rn], wr=[dstq])
                    for c in range(4):
                        yield
                        pb = ps[c % 2]
                        proj(2944 + c * 128, 128, pb)
                        kb.op("act", lambda e: e.activation(out=zs[c][:], in_=pb[:, 0:TW], func=AF.Silu), rd=[pb],
                              wr=[zs[c]])


                def g_attn(i):
                    mix = mix2[i % 2]
                    for s in range(2 if DEBUG_STAGE >= 1 else 0):
                        gblk = i * 2 + s
                        for kv in range(2):
                            pr = slice(kv * 64, (kv + 1) * 64)
                            sc_, sp_b, po, pd = ps[0], ps[1], ps[2], ps[3]
                            qv = qTt[pr, :, s * 128:(s + 1) * 128]
                            kb.op("pe", lambda e: e.matmul(
                                sc_[:, :].rearrange("p (j q) -> p j q", j=4), lhsT=kTt[pr, 128 + s * 128:256 + s * 128],
                                rhs=qv, start=True, stop=True), rd=[kTt, qTt], wr=[sc_])
                            kb.op("act", lambda e: e.activation(out=Ec[:], in_=sc_[:], func=AF.Exp, scale=0.125),
                                  rd=[sc_], wr=[Ec])
                            kb.op("pool", lambda e: e.tensor_tensor(out=Ec[:], in0=Ec[:], in1=maskcb[:], op=ALU.mult),
                                  rd=[Ec, maskcb], wr=[Ec])
                            yield
                            hasp = gblk > 0
                            if hasp:
                                kb.op("pe", lambda e: e.matmul(
                                    sp_b[:, :].rearrange("p (j q) -> p j q", j=4), lhsT=kTt[pr, s * 128:128 + s * 128],
                                    rhs=qv, start=True, stop=True), rd=[kTt, qTt], wr=[sp_b])
                                kb.op("act", lambda e: e.activation(out=Ep[:], in_=sp_b[:], func=AF.Exp, scale=0.125),
                                      rd=[sp_b], wr=[Ep])
                                kb.op("pool", lambda e: e.tensor_tensor(out=Ep[:], in0=Ep[:], in1=maskpb[:],
                                                                        op=ALU.mult), rd=[Ep, maskpb], wr=[Ep])
                            yield
                            kb.op("pe", lambda e: e.matmul(po[:], lhsT=vtm[:, 1 + s, :], rhs=Ec[:], start=True,
                                                           stop=not hasp), rd=[vtm, Ec], wr=[po])
                            if hasp:
                                kb.op("pe", lambda e: e.matmul(po[:], lhsT=vtm[:, s, :], rhs=Ep[:], start=False,
                                                               stop=True), rd=[vtm, Ep], wr=[po])
                            kb.op("pe", lambda e: e.matmul(pd[:], lhsT=onesb[:], rhs=Ec[:], start=True,
                                                           stop=not hasp), rd=[onesb, Ec], wr=[pd])
                            if hasp:
                                kb.op("pe", lambda e: e.matmul(pd[:], lhsT=onesb[:], rhs=Ep[:], start=False,
                                                               stop=True), rd=[onesb, Ep], wr=[pd])
                            yield
                            for g_ in range(4):
                                kb.op("dve", lambda e: e.tensor_scalar(
                                    out=rden[pr, g_ * 128:(g_ + 1) * 128], in0=pd[pr, g_ * 128:(g_ + 1) * 128],
                                    scalar1=esink[pr, kv * 4 + g_:kv * 4 + g_ + 1], scalar2=None, op0=ALU.add),
                                    rd=[pd, esink], wr=[rden])
                            kb.op("dve", lambda e: e.reciprocal(out=rden[pr, :], in_=rden[pr, :]), rd=[rden],
                                  wr=[rden])
                            kb.op("dve", lambda e: e.tensor_tensor(
                                out=mix[pr, 0:4, s * 128:(s + 1) * 128],
                                in0=po[pr, :].rearrange("p (j q) -> p j q", j=4),
                                in1=rden[pr, :].rearrange("p (j q) -> p j q", j=4), op=ALU.mult),
                                rd=[po, rden], wr=[mix])
                    kb.op("pool", lambda e: e.tensor_copy(out=kTt[:, 0:128], in_=kTt[:, TW:TW + 128]), rd=[kTt],
                          wr=[kTt])
                    kb.op("pool", lambda e: e.tensor_copy(out=vtm[:, 0, :], in_=vtm[:, 2, :]), rd=[vtm], wr=[vtm])

                    yield

                def g_gdn(i):
                    mix = mix2[i % 2]
                    qbT, kbT, vbT, zs, sc = qbT2[i % 2], kbT2[i % 2], vbT2[i % 2], zs2[i % 2], sc2[i % 2]
                    for s in range(2 if DEBUG_STAGE >= 2 else 0):
                        c_ = sc[s]
                        tok = slice(s * 128, (s + 1) * 128)
                        pA, pB, pC, pD = ps[4], ps[5], ps[6], ps[7]
                        pI = (ps[4], ps[5], ps[6], ps[7])
                        yield
                        kb.op("pe", lambda e: e.matmul(pA[:, 0:4], lhsT=tri[:], rhs=c_[:, 8:12], start=True, stop=True),
                              rd=[tri, c_], wr=[pA])
                        kb.op("pe", lambda e: e.matmul(pB[:, 0:4], lhsT=onesf[:], rhs=c_[:, 8:12], start=True,
                                                       stop=True), rd=[onesf, c_], wr=[pB])
                        kb.op("act", lambda e: e.activation(out=c_[:, 12:16], in_=pA[:, 0:4], func=AF.Copy), rd=[pA],
                              wr=[c_])
                        kb.op("act", lambda e: e.activation(out=c_[:, 20:24], in_=pB[:, 0:4], func=AF.Copy), rd=[pB],
                              wr=[c_])
                        kb.op("act", lambda e: e.activation(out=c_[:, 16:20], in_=c_[:, 12:16], func=AF.Exp), rd=[c_],
                              wr=[c_])
                        kb.op("act", lambda e: e.activation(out=c_[:, 24:28], in_=c_[:, 20:24], func=AF.Exp), rd=[c_],
                              wr=[c_])
                        kb.op("dve", lambda e: e.tensor_tensor(out=c_[:, 28:32], in0=c_[:, 20:24], in1=c_[:, 12:16],
                                                               op=ALU.subtract), rd=[c_], wr=[c_])
                        kb.op("act", lambda e: e.activation(out=c_[:, 28:32], in_=c_[:, 28:32], func=AF.Exp), rd=[c_],
                              wr=[c_])
                        yield
                        for h in range(4):
                            kb.op("dve", lambda e: e.tensor_scalar(out=dgb[:, h * 128:(h + 1) * 128], in0=ident[:],
                                                                   scalar1=c_[:, 12 + h:13 + h], scalar2=None,
                                                                   op0=ALU.mult), rd=[ident, c_], wr=[dgb])
                        kb.op("pe", lambda e: e.matmul(pA[:], lhsT=onesf[:], rhs=dgb[:], start=True, stop=True),
                              rd=[onesf, dgb], wr=[pA])
                        for h in range(4):
                            kb.op("dve", lambda e: e.tensor_scalar(out=tE[:, h * 128:(h + 1) * 128],
                                                                   in0=pA[:, h * 128:(h + 1) * 128],
                                                                   scalar1=c_[:, 12 + h:13 + h], scalar2=None,
                                                                   op0=ALU.subtract), rd=[pA, c_], wr=[tE])
                        kb.op("pool", lambda e: e.tensor_tensor(out=tE[:], in0=tE[:], in1=mk[2][:], op=ALU.add),
                              rd=[tE, mk[2]], wr=[tE])
                        kb.op("act", lambda e: e.activation(out=tE[:], in_=tE[:], func=AF.Exp), rd=[tE], wr=[tE])
                        for h in range(4):
                            kb.op("dve", lambda e: e.tensor_scalar(out=dgb[:, h * 128:(h + 1) * 128], in0=ident[:],
                                                                   scalar1=c_[:, h:h + 1], scalar2=None,
                                                                   op0=ALU.mult), rd=[ident, c_], wr=[dgb])
                        kb.op("pe", lambda e: e.matmul(pB[:], lhsT=onesf[:], rhs=dgb[:], start=True, stop=True),
                              rd=[onesf, dgb], wr=[pB])
                        yield
                        for h in range(4):
                            kb.op("pe", lambda e: e.matmul(pC[:, h * 128:(h + 1) * 128], lhsT=kbT[h][:, tok],
                                                           rhs=kbT[h][:, tok], start=True, stop=True),
                                  rd=[kbT[h]], wr=[pC])
                            kb.op("pe", lambda e: e.matmul(pD[:, h * 128:(h + 1) * 128], lhsT=kbT[h][:, tok],
                                                           rhs=qbT[h][:, tok], start=True, stop=True),
                                  rd=[kbT[h], qbT[h]], wr=[pD])
                        kb.op("dve", lambda e: e.tensor_tensor(out=QKD[:], in0=pD[:], in1=tE[:], op=ALU.mult),
                              rd=[pD, tE], wr=[QKD])
                        U0, L0 = Um[0], Lm[0]
                        kb.op("dve", lambda e: e.tensor_tensor(out=U0[:], in0=pC[:], in1=tE[:], op=ALU.mult),
                              rd=[pC, tE], wr=[U0])
                        kb.op("dve", lambda e: e.scalar_tensor_tensor(out=U0[:], in0=U0[:], scalar=-1.0, in1=pB[:],
                                                                      op0=ALU.mult, op1=ALU.mult),
                              rd=[U0, pB], wr=[U0])
                        kb.op("pool", lambda e: e.tensor_tensor(out=U0[:], in0=U0[:], in1=mk[3][:], op=ALU.mult),
                              rd=[U0, mk[3]], wr=[U0])
                        yield
                        for h in range(4):
                            kb.op("pe", lambda e: e.transpose(out=pI[0][:, h * 128:(h + 1) * 128],
                                                              in_=U0[:, h * 128:(h + 1) * 128], identity=ident[:]),
                                  rd=[U0, ident], wr=[pI[0]])
                        kb.op("act", lambda e: e.activation(out=L0[:], in_=pI[0][:], func=AF.Copy), rd=[pI[0]],
                              wr=[L0])
                        yield
                        for h in range(4):
                            kb.op("pool", lambda e: e.tensor_tensor(out=Rm[:, h * 128:(h + 1) * 128],
                                                                    in0=U0[:, h * 128:(h + 1) * 128], in1=ident[:],
                                                                    op=ALU.add), rd=[U0, ident], wr=[Rm])
                        cu, cl = 0, 0
                        for m in range(1, 7):
                            Up, Lp = Um[cu], Lm[cl]
                            Un, Ln = Um[1 - cu], Lm[1 - cl]
                            yield
                            if m < 6:
                                for h in range(4):
                                    hs = slice(h * 128, (h + 1) * 128)
                                    kb.op("pe", lambda e: e.matmul(pI[1][:, hs], lhsT=Lp[:, hs], rhs=Up[:, hs],
                                                                   start=True, stop=True), rd=[Lp, Up], wr=[pI[1]])
                            for h in range(4):
                                hs = slice(h * 128, (h + 1) * 128)
                                kb.op("pe", lambda e: e.matmul(pI[2][:, hs], lhsT=Up[:, hs], rhs=Lp[:, hs],
                                                               start=True, stop=True), rd=[Lp, Up], wr=[pI[2]])
                            if m < 6:
                                kb.op("act", lambda e: e.activation(out=Un[:], in_=pI[1][:], func=AF.Copy),
                                      rd=[pI[1]], wr=[Un])
                            kb.op("dve", lambda e: e.tensor_copy(out=Ln[:], in_=pI[2][:]), rd=[pI[2]], wr=[Ln])
                            yield
                            for h in range(4):
                                hs = slice(h * 128, (h + 1) * 128)
                                kb.op("pe", lambda e: e.matmul(pI[3][:, hs], lhsT=Ln[:, hs], rhs=Rm[:, hs],
                                                               start=True, stop=True), rd=[Ln, Rm], wr=[pI[3]])
                            if m < 6:
                                kb.op("dve", lambda e: e.tensor_tensor(out=Rm[:], in0=Rm[:], in1=pI[3][:],
                                                                       op=ALU.add), rd=[Rm, pI[3]], wr=[Rm])
                            else:
                                kb.op("dve", lambda e: e.tensor_tensor(out=TT[:], in0=Rm[:], in1=pI[3][:],
                                                                       op=ALU.add), rd=[Rm, pI[3]], wr=[TT])
                            cu, cl = 1 - cu, 1 - cl
                        yield
                        for h in range(4):
                            hs = slice(h * 128, (h + 1) * 128)
                            kb.op("pe", lambda e: e.matmul(pA[:, hs], lhsT=kbT[h][:, tok], rhs=identb[:], start=True,
                                                           stop=True), rd=[kbT[h], identb], wr=[pA])
                            kb.op("pe", lambda e: e.matmul(pB[:, hs], lhsT=vbT[h][:, tok], rhs=identb[:], start=True,
                                                           stop=True), rd=[vbT[h], identb], wr=[pB])
                        kb.op("act", lambda e: e.activation(out=kTM[:], in_=pA[:], func=AF.Copy), rd=[pA], wr=[kTM])
                        kb.op("act", lambda e: e.activation(out=vTM[:], in_=pB[:], func=AF.Copy), rd=[pB], wr=[vTM])
                        yield
                        for h in range(4):
                            hs = slice(h * 128, (h + 1) * 128)
                            kb.op("pe", lambda e: e.matmul(pC[:, hs], lhsT=kbT[h][:, tok], rhs=Sb[:, hs], start=True,
                                                           stop=True), rd=[kbT[h], Sb], wr=[pC])
                        for h in range(4):
                            hs = slice(h * 128, (h + 1) * 128)
                            kb.op("dve", lambda e: e.scalar_tensor_tensor(
                                out=P2sb[:, hs], in0=pC[:, hs], scalar=c_[:, 16 + h:17 + h], in1=vTM[:, hs],
                                op0=ALU.mult, op1=ALU.subtract), rd=[pC, c_, vTM], wr=[P2sb])
                            kb.op("dve", lambda e: e.tensor_scalar(out=rhsv[:, hs], in0=P2sb[:, hs],
                                                                   scalar1=c_[:, 4 + h:5 + h], scalar2=None,
                                                                   op0=ALU.mult), rd=[P2sb, c_], wr=[rhsv])
                        yield
                        for h in range(4):
                            hs = slice(h * 128, (h + 1) * 128)
                            kb.op("pe", lambda e: e.matmul(pD[:, hs], lhsT=TT[:, hs], rhs=rhsv[:, hs], start=True,
                                                           stop=True), rd=[TT, rhsv], wr=[pD])
                        kb.op("act", lambda e: e.activation(out=vn[:], in_=pD[:], func=AF.Copy), rd=[pD], wr=[vn])
                        for h in range(4):
                            hs = slice(h * 128, (h + 1) * 128)
                            kb.op("act", lambda e: e.activation(out=vnd[:, hs], in_=pD[:, hs], func=AF.Identity,
                                                                scale=c_[:, 28 + h:29 + h]), rd=[pD, c_], wr=[vnd])
                        yield
                        for h in range(4):
                            hs = slice(h * 128, (h + 1) * 128)
                            kb.op("pe", lambda e: e.matmul(pA[:, hs], lhsT=qbT[h][:, tok], rhs=Sb[:, hs], start=True,
                                                           stop=True), rd=[qbT[h], Sb], wr=[pA])
                            kb.op("pe", lambda e: e.matmul(pB[:, hs], lhsT=QKD[:, hs], rhs=vn[:, hs], start=True,
                                                           stop=True), rd=[QKD, vn], wr=[pB])
                        kb.op("act", lambda e: e.activation(out=P2sb[:], in_=pB[:], func=AF.Copy), rd=[pB], wr=[P2sb])
                        for h in range(4):
                            hs = slice(h * 128, (h + 1) * 128)
                            kb.op("dve", lambda e: e.scalar_tensor_tensor(
                                out=ob[:, hs], in0=pA[:, hs], scalar=c_[:, 16 + h:17 + h], in1=P2sb[:, hs],
                                op0=ALU.mult, op1=ALU.add), rd=[pA, c_, P2sb], wr=[ob])
                        yield
                        for h in range(4):
                            hs = slice(h * 128, (h + 1) * 128)
                            kb.op("pe", lambda e: e.matmul(pC[:, hs], lhsT=kTM[:, hs], rhs=vnd[:, hs], start=True,
                                                           stop=True), rd=[kTM, vnd], wr=[pC])
                        for h in range(4):
                            hs = slice(h * 128, (h + 1) * 128)
                            kb.op("dve", lambda e: e.scalar_tensor_tensor(
                                out=S[:, hs], in0=S[:, hs], scalar=c_[:, 24 + h:25 + h], in1=pC[:, hs],
                                op0=ALU.mult, op1=ALU.add), rd=[S, c_, pC], wr=[S])
                        kb.op("act", lambda e: e.activation(out=Sb[:], in_=S[:], func=AF.Copy), rd=[S], wr=[Sb])
                        yield
                        for h in range(4):
                            hs = slice(h * 128, (h + 1) * 128)
                            kb.op("act", lambda e: e.activation(out=osq[:], in_=ob[:, hs], func=AF.Square,
                                                                accum_out=c_[:, 32 + h:33 + h]), rd=[ob],
                                  wr=[osq, c_])
                        kb.op("pool", lambda e: e.tensor_scalar(out=c_[:, 36:40], in0=c_[:, 32:36],
                                                                scalar1=1.0 / 128, scalar2=NORM_EPS, op0=ALU.mult,
                                                                op1=ALU.add), rd=[c_], wr=[c_])
                        kb.op("pool", lambda e: e.tensor_tensor(out=c_[:, 36:40], in0=c_[:, 36:40],
                                                                in1=mhalf4[:],
                                                                op=ALU.pow), rd=[c_, mhalf4], wr=[c_])
                        for h in range(4):
                            hs = slice(h * 128, (h + 1) * 128)
                            kb.op("dve", lambda e: e.tensor_scalar(out=ob[:, hs], in0=ob[:, hs],
                                                                   scalar1=c_[:, 36 + h:37 + h], scalar2=None,
                                                                   op0=ALU.mult), rd=[ob, c_], wr=[ob])
                            kb.op("pe", lambda e: e.transpose(out=pD[:, hs], in_=ob[:, hs], identity=ident[:]),
                                  rd=[ob, ident], wr=[pD])
                        for h in range(4):
                            hs = slice(h * 128, (h + 1) * 128)
                            kb.op("dve", lambda e: e.scalar_tensor_tensor(
                                out=mix[:, 4 + h, tok], in0=pD[:, hs], scalar=nw[:, 0:1], in1=zs[h][:, tok],
                                op0=ALU.mult, op1=ALU.mult), rd=[pD, nw, zs[h]], wr=[mix])
                    yield

                def run_rr(gens):
                    gens = list(gens)
                    while gens:
                        for g_ in list(gens):
                            try:
                                next(g_)
                            except StopIteration:
                                gens.remove(g_)

                def g_out(i):
                    yield from outproj_ln1(l, i, mix2[i % 2], wov, WO, x_src, Bx_src, nsub=2)

                run_rr([g_proj(0)])
                for i in range(NTH):
                    pa = [g_attn(i)]
                    if i > 0:
                        pa.append(g_out(i - 1))
                    run_rr(pa)
                    pb_ = [g_gdn(i)]
                    if i + 1 < NTH:
                        pb_.append(g_proj(i + 1))
                    run_rr(pb_)
                run_rr([g_out(NTH - 1)])
                kb.barrier()

        def sp_chain(c_, a, w, dst):
            y = c_[:, a:a + 4]
            t = c_[:, w:w + 4]
            d = c_[:, dst:dst + 4]

            def vop(fn):
                kb.op("dve", fn, rd=[c_], wr=[c_])

            vop(lambda e: e.tensor_scalar(out=t, in0=y, scalar1=-1.0, scalar2=None, op0=ALU.mult))
            vop(lambda e: e.tensor_tensor(out=t, in0=t, in1=y, op=ALU.min))
            kb.op("act", lambda e: e.activation(out=t, in_=t, func=AF.Exp), rd=[c_], wr=[c_])
            vop(lambda e: e.tensor_scalar(out=d, in0=t, scalar1=2.0, scalar2=None, op0=ALU.add))
            vop(lambda e: e.reciprocal(out=d, in_=d))
            vop(lambda e: e.tensor_tensor(out=t, in0=t, in1=d, op=ALU.mult))
            vop(lambda e: e.tensor_tensor(out=d, in0=t, in1=t, op=ALU.mult))
            p_ = c_[:, 16:20]
            vop(lambda e: e.tensor_scalar(out=p_, in0=d, scalar1=1.0 / 15, scalar2=1.0 / 13, op0=ALU.mult,
                                          op1=ALU.add))
            for cc in (11, 9, 7, 5, 3, 1):
                vop(lambda e: e.tensor_tensor(out=p_, in0=p_, in1=d, op=ALU.mult))
                vop(lambda e: e.tensor_scalar(out=p_, in0=p_, scalar1=1.0 / cc, scalar2=None, op0=ALU.add))
            vop(lambda e: e.tensor_tensor(out=p_, in0=p_, in1=t, op=ALU.mult))
            vop(lambda e: e.tensor_scalar(out=d, in0=y, scalar1=0.0, scalar2=None, op0=ALU.max))
            vop(lambda e: e.scalar_tensor_tensor(out=d, in0=p_, scalar=2.0, in1=d, op0=ALU.mult, op1=ALU.add))

        pass_t0()
        jh = jr = 0
        for l, kind in enumerate(kinds):
            x_src, Bx = (x_in, None) if l == 0 else (xres_d, B_xres)
            if kind == "rec":
                pass_rec(l, jr, x_src, Bx)
                jr += 1
            else:
                pass_hyb(l, jh, x_src, Bx)
                jh += 1
            pass_mlp(l, 0, False)
            pass_mlp(l, 1, l == L - 1)
        kb.finish()
        global LAST_CNT
        LAST_CNT = dict(kb.cnt)
        LAST_CNT['dma'] = {q: p['i'] for q, p in kb.dpool.items()}
    return nc


def prep_weights(inp, kinds, T):
    f32 = np.float32
    L = len(kinds)
    hyb_idx = [l // 2 for l, k in enumerate(kinds) if k == "hyb"]
    rec_idx = [l // 2 for l, k in enumerate(kinds) if k == "rec"]
    w = {}
    w["ident"] = np.eye(128, dtype=f32)
    ri = rec_idx or [0]
    w["rec_w_in"] = np.ascontiguousarray(inp["rec_w_in"][ri])
    w["rec_w_out"] = np.ascontiguousarray(inp["rec_w_out"][ri])
    w["rec_w_ax"] = np.ascontiguousarray(np.stack([inp["rec_w_a"][ri], inp["rec_w_x"][ri]], axis=1))
    vec = np.stack([inp["rec_conv_w"][ri][:, 0], inp["rec_conv_w"][ri][:, 1], inp["rec_conv_w"][ri][:, 2],
                    inp["rec_conv_w"][ri][:, 3], inp["rec_conv_b"][ri], inp["rec_b_a"][ri], inp["rec_b_x"][ri],
                    inp["rec_lambda"][ri]], axis=-1)
    w["rec_vec"] = np.ascontiguousarray(vec.reshape(len(ri), 8, 128, 8).transpose(0, 2, 1, 3))
    lay = list(range(L))
    gbs = np.stack([inp["ln1_g"][lay], inp["ln1_b"][lay], inp["ln2_g"][lay], inp["ln2_b"][lay]], axis=1)
    w["ln_gb"] = np.ascontiguousarray(np.broadcast_to(gbs[:, :, None, :], (L, 4, 128, D))).astype(f32)
    w["mlp_w1"] = np.ascontiguousarray(inp["mlp_w1"][lay])
    w["mlp_w2"] = np.ascontiguousarray(inp["mlp_w2"][lay])
    hi = hyb_idx or [0]
    qperm = np.concatenate([np.r_[jj * 64:(jj + 1) * 64, (4 + jj) * 64:(5 + jj) * 64] for jj in range(4)])
    sw64 = np.r_[32:64, 0:32]
    qswap = np.concatenate([h * 64 + sw64 for h in range(8)])[qperm]
    kswap = 512 + np.concatenate([h * 64 + sw64 for h in range(2)])
    cols = np.concatenate([qperm, np.arange(512, 640), qswap, kswap, np.arange(640, 768),
                           np.arange(768, 768 + 1536), np.arange(2304, 2816), np.arange(2816, 2824)])
    assert cols.shape[0] == HYBW
    w["hyb_w_in"] = np.ascontiguousarray(inp["hyb_w_in"][hi][:, :, cols])
    rows = np.concatenate([qperm, np.arange(512, 1024)])
    w["hyb_w_out"] = np.ascontiguousarray(inp["hyb_w_out"][hi][:, rows, :])
    cwv = inp["hyb_conv_w"][hi]
    w["hyb_convw"] = np.ascontiguousarray(cwv.reshape(len(hi), 4, 12, 128).transpose(0, 3, 2, 1))
    sm = np.concatenate([inp["hyb_sinks"][hi], inp["hyb_a_log"][hi], inp["hyb_dt_bias"][hi]], axis=1)
    w["hyb_small"] = np.ascontiguousarray(np.broadcast_to(sm[:, None, :], (len(hi), 128, 16))).astype(f32)
    w["hyb_normw"] = np.ascontiguousarray(inp["hyb_norm_w"][hi][:, :, None])
    half = 32
    inv_freq = (f32(10000.0) ** (-np.arange(half, dtype=f32) / f32(half))).astype(f32)
    ang = (np.arange(T, dtype=f32)[None, :] * inv_freq[:, None]).astype(f32)
    cosv, sinv = np.cos(ang).astype(f32), np.sin(ang).astype(f32)
    cos128 = np.concatenate([cosv, cosv, cosv, cosv], axis=0)
    sin128 = np.concatenate([-sinv, sinv, -sinv, sinv], axis=0)
    w["rope_cs"] = np.ascontiguousarray(np.stack([cos128, sin128], axis=0))
    jj_, ii_ = np.meshgrid(np.arange(128), np.arange(128), indexing="ij")
    m_c = (jj_ <= ii_).astype(f32)
    m_p = (jj_ > ii_).astype(f32)
    m_neg = np.where(ii_ >= jj_, 0.0, -30000.0).astype(f32)
    m_strict = (ii_ > jj_).astype(f32)
    w["masks"] = np.ascontiguousarray(np.stack([np.tile(m, (1, 4)) for m in (m_c, m_p, m_neg, m_strict)], axis=0))
    w["tri"] = np.ascontiguousarray((jj_ <= ii_).astype(f32))
    return w


_CACHE = {}


def kernel(**inputs):
    kinds = ["hyb", "rec", "hyb", "rec"]
    T = SEQ
    x = np.asarray(inputs["x"], dtype=np.float32)
    inp = {k: np.asarray(v, dtype=np.float32) for k, v in inputs.items()}
    w = prep_weights(inp, kinds, T)
    if "nc" not in _CACHE:
        _CACHE["nc"] = build_program(T, kinds)
    nc = _CACHE["nc"]
    in_maps = []
    for b in range(NB):
        m = dict(w)
        m["x"] = np.ascontiguousarray(x[b])
        in_maps.append(m)
    res = run_bass_kernel_spmd(nc, in_maps, core_ids=list(range(NB)))
    return np.stack([np.asarray(r["out"], dtype=np.float32) for r in res.results], axis=0)
```
